# Optimizing a Trainium2 kernel written in Bass

```python
import jax, jax.numpy as jnp
from jax import lax
import numpy as np

D_MODEL = 2048
BATCH = 2
SEQ = 8192
DEPTH = 2
DEC_BATCH = 4
DEC_SEQ = 8192
PAST_LEN = 128

GRID_W = 64
Q_BLOCK = 128
D_FF = 5632
NORM_EPS = 1e-6
ROPE_BASE = 10000.0
A_HEADS = 8
A_KV_HEADS = 2
A_HEAD_DIM = 128
B_HEADS = 8
B_Q_LORA = 512
B_KV_LORA = 256
B_NOPE_DIM = 128
B_ROPE_DIM = 64
B_V_DIM = 128
POOL_WINDOWS = (2, 4, 8, 16)
POOL_GROUP = D_MODEL // 4
IN_SIZES = (A_HEADS * A_HEAD_DIM, A_KV_HEADS * A_HEAD_DIM, A_KV_HEADS * A_HEAD_DIM, B_Q_LORA, B_KV_LORA, B_ROPE_DIM)
IN_COLS = 1024 + 256 + 256 + 512 + 256 + 64
MIX_WIDTH = A_HEADS * A_HEAD_DIM + B_HEADS * B_V_DIM

kernel_name = 'hybrid_gqa_mla_pool_macaron_encoder'


def rms_norm(x, g):
    xf = x.astype(jnp.float32)
    y = xf * lax.rsqrt(jnp.mean(xf * xf, axis=-1, keepdims=True) + NORM_EPS)
    return (y * g.astype(jnp.float32)).astype(x.dtype)


def swiglu(x, w_gate, w_up, w_down):
    return (jax.nn.silu(x @ w_gate) * (x @ w_up)) @ w_down


def axial_rope_angles(seq_len, rot_dim):
    rows = seq_len // GRID_W
    row = jnp.repeat(jnp.arange(rows, dtype=jnp.float32), GRID_W)
    col = jnp.tile(jnp.arange(GRID_W, dtype=jnp.float32), rows)
    half = rot_dim // 2
    freqs = ROPE_BASE ** (-jnp.arange(0, half, 2, dtype=jnp.float32) / half)
    ang = jnp.concatenate([row[:, None] * freqs, col[:, None] * freqs], axis=-1)
    return jnp.cos(ang), jnp.sin(ang)


def apply_rope(x, cos, sin):
    xp = x.astype(jnp.float32).reshape(*x.shape[:-1], x.shape[-1] // 2, 2)
    x0, x1 = xp[..., 0], xp[..., 1]
    c = cos[None, :, None, :]
    s = sin[None, :, None, :]
    out = jnp.stack([x0 * c - x1 * s, x0 * s + x1 * c], axis=-1)
    return out.reshape(x.shape).astype(x.dtype)


def blocked_attention(q, k, v, scale):
    b, s, h, dq = q.shape
    hkv = k.shape[2]
    g = h // hkv
    dv = v.shape[-1]
    nb = s // Q_BLOCK
    qb = q.reshape(b, nb, Q_BLOCK, hkv, g, dq).transpose(1, 0, 2, 3, 4, 5)

    def one_block(qblk):
        sc = jnp.einsum('bqhgd,bkhd->bhgqk', qblk, k, preferred_element_type=jnp.float32) * scale
        p = jax.nn.softmax(sc, axis=-1).astype(v.dtype)
        return jnp.einsum('bhgqk,bkhe->bqhge', p, v)

    o = lax.map(one_block, qb)
    return o.transpose(1, 0, 2, 3, 4, 5).reshape(b, s, h * dv)


def parallel_attention_mixer(h, w_in, a_q_norm_g, a_k_norm_g, b_cq_norm_g, b_w_uq, b_ckv_norm_g, b_w_ukv, w_out, rope_a, rope_b):
    b, s, _ = h.shape
    offs = [int(v) for v in np.cumsum(IN_SIZES)[:-1]]
    qa, ka, va, cq, ckv, kr = jnp.split(h @ w_in, offs, axis=-1)
    cos_a, sin_a = rope_a
    qa = apply_rope(rms_norm(qa.reshape(b, s, A_HEADS, A_HEAD_DIM), a_q_norm_g), cos_a, sin_a)
    ka = apply_rope(rms_norm(ka.reshape(b, s, A_KV_HEADS, A_HEAD_DIM), a_k_norm_g), cos_a, sin_a)
    va = va.reshape(b, s, A_KV_HEADS, A_HEAD_DIM)
    oa = blocked_attention(qa, ka, va, A_HEAD_DIM ** -0.5)
    cos_b, sin_b = rope_b
    qb = (rms_norm(cq, b_cq_norm_g) @ b_w_uq).reshape(b, s, B_HEADS, B_NOPE_DIM + B_ROPE_DIM)
    q_nope, q_rope = qb[..., :B_NOPE_DIM], qb[..., B_NOPE_DIM:]
    q_rope = apply_rope(q_rope, cos_b, sin_b)
    kv = (rms_norm(ckv, b_ckv_norm_g) @ b_w_ukv).reshape(b, s, B_HEADS, B_NOPE_DIM + B_V_DIM)
    k_nope, vb = kv[..., :B_NOPE_DIM], kv[..., B_NOPE_DIM:]
    k_rope = apply_rope(kr[:, :, None, :], cos_b, sin_b)
    q_full = jnp.concatenate([q_nope, q_rope], axis=-1)
    k_full = jnp.concatenate([k_nope, jnp.broadcast_to(k_rope, (b, s, B_HEADS, B_ROPE_DIM))], axis=-1)
    ob = blocked_attention(q_full, k_full, vb, (B_NOPE_DIM + B_ROPE_DIM) ** -0.5)
    return jnp.concatenate([oa, ob], axis=-1) @ w_out


def multiscale_pool_mixer(h, pool_w, pool_scale):
    b, s, _ = h.shape
    t = jnp.arange(s)
    outs = []
    for gi, w in enumerate(POOL_WINDOWS):
        xf = h[..., gi * POOL_GROUP:(gi + 1) * POOL_GROUP].astype(jnp.float32)
        cs = jnp.concatenate([jnp.zeros((b, 1, POOL_GROUP), jnp.float32), jnp.cumsum(xf, axis=1)], axis=1)
        lo = jnp.clip(t - w // 2, 0, s)
        hi = jnp.clip(t + w // 2, 0, s)
        mean = (cs[:, hi] - cs[:, lo]) / (hi - lo).astype(jnp.float32)[None, :, None]
        pooled = (mean - xf).astype(h.dtype)
        outs.append(pooled @ pool_w[gi])
    return jnp.concatenate(outs, axis=-1) * pool_scale


def ffn_half(x, pre_g, w_gate, w_up, w_down, post_g):
    return x + 0.5 * rms_norm(swiglu(rms_norm(x, pre_g), w_gate, w_up, w_down), post_g)


def run_trunk(x, layers):
    s = x.shape[1]
    rope_a = axial_rope_angles(s, A_HEAD_DIM)
    rope_b = axial_rope_angles(s, B_ROPE_DIM)
    for i in range(DEPTH):
        p = layers[i]
        x = ffn_half(x, *p['ffn1'])
        pre_g, *mix_w, post_g = p['mix']
        hn = rms_norm(x, pre_g)
        if i % 2 == 0:
            m = parallel_attention_mixer(hn, *mix_w, rope_a, rope_b)
        else:
            m = multiscale_pool_mixer(hn, *mix_w)
        x = x + rms_norm(m, post_g)
        x = ffn_half(x, *p['ffn2'])
    return x


def setup_inputs(seed: int = 0) -> dict:
    key = jax.random.key(seed)
    ki = iter(jax.random.split(key, 64))

    def dense(shape, fan_in):
        return jax.random.normal(next(ki), shape, jnp.float32) * fan_in ** -0.5

    def gain(n):
        return 1.0 + 0.05 * jax.random.normal(next(ki), (n,), jnp.float32)

    def ffn(prefix):
        return {prefix + '_pre_g': gain(D_MODEL),
                prefix + '_w_gate': dense((D_MODEL, D_FF), D_MODEL),
                prefix + '_w_up': dense((D_MODEL, D_FF), D_MODEL),
                prefix + '_w_down': dense((D_FF, D_MODEL), D_FF),
                prefix + '_post_g': gain(D_MODEL)}

    d = {}
    d['x_prompt'] = jax.random.normal(next(ki), (BATCH, SEQ, D_MODEL), jnp.float32)
    d['x_sample'] = jax.random.normal(next(ki), (DEC_BATCH, DEC_SEQ, D_MODEL), jnp.float32)
    d.update(ffn('l0_ffn1'))
    d['l0_mix_pre_g'] = gain(D_MODEL)
    d['l0_w_in'] = dense((D_MODEL, IN_COLS), D_MODEL)
    d['l0_a_q_norm_g'] = gain(A_HEAD_DIM)
    d['l0_a_k_norm_g'] = gain(A_HEAD_DIM)
    d['l0_b_cq_norm_g'] = gain(B_Q_LORA)
    d['l0_b_w_uq'] = dense((B_Q_LORA, B_HEADS * (B_NOPE_DIM + B_ROPE_DIM)), B_Q_LORA)
    d['l0_b_ckv_norm_g'] = gain(B_KV_LORA)
    d['l0_b_w_ukv'] = dense((B_KV_LORA, B_HEADS * (B_NOPE_DIM + B_V_DIM)), B_KV_LORA)
    d['l0_w_out'] = dense((MIX_WIDTH, D_MODEL), MIX_WIDTH)
    d['l0_mix_post_g'] = gain(D_MODEL)
    d.update(ffn('l0_ffn2'))
    d.update(ffn('l1_ffn1'))
    d['l1_mix_pre_g'] = gain(D_MODEL)
    d['l1_pool_w'] = dense((len(POOL_WINDOWS), POOL_GROUP, POOL_GROUP), POOL_GROUP)
    d['l1_pool_scale'] = 1.0 + 0.1 * jax.random.normal(next(ki), (D_MODEL,), jnp.float32)
    d['l1_mix_post_g'] = gain(D_MODEL)
    d.update(ffn('l1_ffn2'))
    return d


def reference(x_prompt, x_sample,
              l0_ffn1_pre_g, l0_ffn1_w_gate, l0_ffn1_w_up, l0_ffn1_w_down, l0_ffn1_post_g,
              l0_mix_pre_g, l0_w_in, l0_a_q_norm_g, l0_a_k_norm_g, l0_b_cq_norm_g, l0_b_w_uq,
              l0_b_ckv_norm_g, l0_b_w_ukv, l0_w_out, l0_mix_post_g,
              l0_ffn2_pre_g, l0_ffn2_w_gate, l0_ffn2_w_up, l0_ffn2_w_down, l0_ffn2_post_g,
              l1_ffn1_pre_g, l1_ffn1_w_gate, l1_ffn1_w_up, l1_ffn1_w_down, l1_ffn1_post_g,
              l1_mix_pre_g, l1_pool_w, l1_pool_scale, l1_mix_post_g,
              l1_ffn2_pre_g, l1_ffn2_w_gate, l1_ffn2_w_up, l1_ffn2_w_down, l1_ffn2_post_g):
    layers = [
        {'ffn1': (l0_ffn1_pre_g, l0_ffn1_w_gate, l0_ffn1_w_up, l0_ffn1_w_down, l0_ffn1_post_g),
         'mix': (l0_mix_pre_g, l0_w_in, l0_a_q_norm_g, l0_a_k_norm_g, l0_b_cq_norm_g, l0_b_w_uq,
                 l0_b_ckv_norm_g, l0_b_w_ukv, l0_w_out, l0_mix_post_g),
         'ffn2': (l0_ffn2_pre_g, l0_ffn2_w_gate, l0_ffn2_w_up, l0_ffn2_w_down, l0_ffn2_post_g)},
        {'ffn1': (l1_ffn1_pre_g, l1_ffn1_w_gate, l1_ffn1_w_up, l1_ffn1_w_down, l1_ffn1_post_g),
         'mix': (l1_mix_pre_g, l1_pool_w, l1_pool_scale, l1_mix_post_g),
         'ffn2': (l1_ffn2_pre_g, l1_ffn2_w_gate, l1_ffn2_w_up, l1_ffn2_w_down, l1_ffn2_post_g)},
    ]
    y_prompt = run_trunk(x_prompt, layers)
    y_sample = run_trunk(x_sample, layers)
    return (y_prompt, y_sample)
```

```python
import numpy as np
from contextlib import ExitStack
import concourse.bass as bass
import concourse.mybir as mybir
from concourse.bass_utils import run_bass_kernel_spmd

F32 = mybir.dt.float32
BF16 = mybir.dt.bfloat16
AF = mybir.ActivationFunctionType
ALU = mybir.AluOpType

T = 512
D = 2048
DC = 16
DFF = 5632
FC = 44
EPS = 1e-6
GRID_W = 64
IN_COLS = 2368
FFNS = ["l0_ffn1", "l0_ffn2", "l1_ffn1", "l1_ffn2"]

WSHAPES = {}
for _p in FFNS:
    WSHAPES[_p + "_w_gate"] = (D, DFF)
    WSHAPES[_p + "_w_up"] = (D, DFF)
    WSHAPES[_p + "_w_down"] = (DFF, D)
WSHAPES["l0_w_in"] = (D, IN_COLS)
WSHAPES["l0_b_w_uq"] = (512, 1536)
WSHAPES["l0_b_w_ukv"] = (256, 2048)
WSHAPES["l0_w_out"] = (D, D)
WSHAPES["l1_pool_w"] = (4 * 512, 512)

VECS = []
for _p in FFNS:
    VECS += [(_p + "_pre_g", 2048), (_p + "_post_g", 2048)]
VECS += [("l0_mix_pre_g", 2048), ("l0_mix_post_g", 2048), ("l1_mix_pre_g", 2048),
         ("l1_mix_post_g", 2048), ("l1_pool_scale", 2048), ("l0_a_q_norm_g", 128),
         ("l0_a_k_norm_g", 128), ("l0_b_cq_norm_g", 512), ("l0_b_ckv_norm_g", 256)]
VCOL = {}
_c = 0
for _n, _l in VECS:
    VCOL[_n] = _c
    _c += _l // 128
NCV = _c


class Sched:
    def __init__(self, nc, es):
        self.nc, self.es = nc, es
        self.sems = {}
        self.eng = {}
        for name, e in [("pe", nc.tensor), ("act", nc.scalar), ("dve", nc.vector),
                        ("pool", nc.gpsimd), ("sp", nc.sync)]:
            self.sems[name] = es.enter_context(nc.semaphore("sem_" + name))
            self.eng[name] = dict(e=e, cnt=0, seen={})
        self.lastw = {}
        self.readers = {}
        self.dcnt = {}
        self.n = 0

    def _waits(self, en, reads, writes):
        E = self.eng[en]
        need = {}
        for k in reads:
            t = self.lastw.get(k)
            if t is not None and need.get(t[0], 0) < t[1]:
                need[t[0]] = t[1]
        for k in writes:
            t = self.lastw.get(k)
            if t is not None and need.get(t[0], 0) < t[1]:
                need[t[0]] = t[1]
            r = self.readers.get(k)
            if r:
                for s, v in r.items():
                    if need.get(s, 0) < v:
                        need[s] = v
        for s, v in need.items():
            if s == "pe" and en == "pe":
                continue
            if E["seen"].get(s, 0) < v:
                E["e"].wait_ge(self.sems[s], v)
                E["seen"][s] = v
                self.n += 1

    def _record(self, tok, reads, writes):
        for k in reads:
            r = self.readers.setdefault(k, {})
            if r.get(tok[0], 0) < tok[1]:
                r[tok[0]] = tok[1]
        for k in writes:
            self.lastw[k] = tok
            self.readers[k] = {}

    def op(self, en, fn, reads=(), writes=(), inc=True):
        self._waits(en, reads, writes)
        E = self.eng[en]
        ins = fn(E["e"])
        self.n += 1
        if inc:
            E["cnt"] += 1
            ins.then_inc(self.sems[en], 1)
            tok = (en, E["cnt"])
        else:
            tok = (en, E["cnt"] + 1)
        self._record(tok, reads, writes)

    def dma(self, q, out, in_, reads=(), writes=(), key=None):
        self._waits(q, reads, writes)
        if key not in self.sems:
            self.sems[key] = self.es.enter_context(self.nc.semaphore("d_" + str(len(self.sems))))
            self.dcnt[key] = 0
        self.dcnt[key] += 16
        self.eng[q]["e"].dma_start(out=out, in_=in_).then_inc(self.sems[key], 16)
        self.n += 1
        self._record((key, self.dcnt[key]), reads, writes)

    def barrier(self):
        for en, E in self.eng.items():
            for s in self.sems:
                v = self.eng[s]["cnt"] if s in self.eng else self.dcnt[s]
                if s == en or v == 0:
                    continue
                if E["seen"].get(s, 0) < v:
                    E["e"].wait_ge(self.sems[s], v)
                    E["seen"][s] = v
        self.lastw.clear()
        self.readers.clear()


def build(S, stop_after=99, dbg=False):
    NT = S // T
    NKC = S // 128
    nc = bass.Bass("TRN2", target_bir_lowering=False)

    def dram(name, shape, dt, kind="Internal"):
        if dbg and kind == "Internal" and not name.startswith("pk_"):
            kind = "ExternalOutput"
        return nc.dram_tensor(name, list(shape), dt, kind=kind).ap()

    x_in = dram("x", [S, D], F32, "ExternalInput")
    y_out = dram("y", [S, D], F32, "ExternalOutput")
    Wd = {n: dram(n, s, F32, "ExternalInput") for n, s in WSHAPES.items()}
    cvec_d = dram("cvec", [128, NCV], F32, "ExternalInput")
    cmat_d = dram("cmat", [128, 4, 128], F32, "ExternalInput")
    ropeA_d = dram("ropeA", [2, 128, S], F32, "ExternalInput")
    ropeB_d = dram("ropeB", [2, 64, S], F32, "ExternalInput")
    invc_d = dram("invc", [4, 128, S], F32, "ExternalInput")

    Pk = {}
    for p in FFNS:
        Pk[p + "_w_gate"] = dram("pk_" + p + "_g", [FC, 128, DC * 128], BF16)
        Pk[p + "_w_up"] = dram("pk_" + p + "_u", [FC, 128, DC * 128], BF16)
        Pk[p + "_w_down"] = dram("pk_" + p + "_d", [DC, 128, FC * 128], BF16)
    Pk["l0_w_in"] = dram("pk_win", [19, 128, DC * 128], BF16)
    Pk["l0_b_w_uq"] = dram("pk_uq", [8, 128, 4 * 192], BF16)
    Pk["l0_b_w_ukv"] = dram("pk_ukv", [16, 128, 2 * 128], BF16)
    Pk["l0_w_out"] = dram("pk_wout", [16, 128, DC * 128], BF16)
    Pk["l1_pool_w"] = dram("pk_pool", [16, 128, 4 * 128], BF16)
    QA = dram("s_QA", [8, 128, S], BF16)
    KA = dram("s_KA", [2, 128, S], BF16)
    VA = dram("s_VA", [S, 256], BF16)
    QBn = dram("s_QBn", [8, 128, S], BF16)
    QBr = dram("s_QBr", [8, 64, S], BF16)
    KBn = dram("s_KBn", [8, 128, S], BF16)
    KBr = dram("s_KBr", [64, S], BF16)
    VB = dram("s_VB", [S, 1024], BF16)
    ATT = dram("s_ATT", [16, 128, S], BF16)
    X1 = dram("s_X1", [16, 128, S], F32)
    X4 = dram("s_X4", [16, 128, S], F32)
    HN4 = dram("s_HN4", [16, 128, S], F32)

    with ExitStack() as es:
        sc = Sched(nc, es)
        cvec = es.enter_context(nc.sbuf_tensor("cvec_sb", [128, NCV], F32))
        cmat = es.enter_context(nc.sbuf_tensor("cmat_sb", [128, 4, 128], F32))
        onesb = es.enter_context(nc.sbuf_tensor("onesb", [128, 128], BF16))
        PS = [es.enter_context(nc.psum_tensor(f"ps{i}", [128, 512], F32)) for i in range(8)]
        PK = [f"ps{i}" for i in range(8)]
        sc.dma("sp", cvec[:], cvec_d, writes=["cvec"], key="k_c")
        sc.dma("sp", cmat[:], cmat_d, writes=["cmat"], key="k_c")
        sc.op("dve", lambda e: e.tensor_copy(out=onesb[:], in_=cmat[:, 0, :]), reads=["cmat"], writes=["onesb"])
        pg2 = es.enter_context(nc.sbuf_tensor("pg2", [128, DC], F32))
        sc.op("dve", lambda e: e.tensor_tensor(out=pg2[:], in0=cvec[:, VCOL["l1_pool_scale"]:VCOL["l1_pool_scale"] + DC],
                                               in1=cvec[:, VCOL["l1_mix_post_g"]:VCOL["l1_mix_post_g"] + DC], op=ALU.mult),
              reads=["cvec"], writes=["pg2"])
        negh = es.enter_context(nc.sbuf_tensor("negh", [128, T], F32))
        sc.op("pool", lambda e: e.memset(negh[:], -0.5), writes=["negh"])
        ONES = cmat[:, 0, :]
        IDENT = cmat[:, 1, :]
        PERMA = cmat[:, 2, :]
        PERMB = cmat[0:64, 3, 0:64]

        def gv(name, c):
            return cvec[:, VCOL[name] + c: VCOL[name] + c + 1]

        def slab_list(src, dst, K, M, CW, cap, row0=0, dch0=0):
            nk = K // 128
            nch = (M + CW - 1) // CW
            G = max(1, min(nch, cap // (nk * CW)))
            out = []
            c0 = 0
            while c0 < nch:
                g = min(G, nch - c0)
                w = min(M, (c0 + g) * CW) - c0 * CW
                if w < g * CW and g > 1:
                    g -= 1
                    w = g * CW
                out.append(dict(src=src, dst=dst, K=K, CW=CW, nk=nk, g=g, w=w, c0=c0, row0=row0, dch0=dch0))
                c0 += g
            return out

        def slab_load(sl, stg_t, key):
            nk, w = sl["nk"], sl["w"]
            sv = stg_t[:, 0:nk * w].rearrange("p (k m) -> p k m", k=nk)
            srcv = sl["src"][sl["row0"]:sl["row0"] + sl["K"], :].rearrange("(k p) m -> p k m", p=128)[:, :, sl["c0"] * sl["CW"]:sl["c0"] * sl["CW"] + w]
            sc.dma("sp", sv, srcv, writes=[key], key="k_" + key)

        def slab_cast(sl, stg_t, skey, stb_t, bkey, en, pieces=1):
            nk, w, g, CW = sl["nk"], sl["w"], sl["g"], sl["CW"]
            cw = w // g
            sv = stg_t[:, 0:nk * w].rearrange("p (k m) -> p k m", k=nk)
            ov = stb_t[:, 0:g * nk * CW].rearrange("p (g k c) -> p g k c", g=g, k=nk)[:, :, :, 0:cw]
            iv = sv.rearrange("p k (g c) -> p g k c", g=g)
            pieces = min(pieces, nk)
            step = (nk + pieces - 1) // pieces
            for k0 in range(0, nk, step):
                k1 = min(nk, k0 + step)
                o_, i_ = ov[:, :, k0:k1, :], iv[:, :, k0:k1, :]
                if en == "act":
                    sc.op("act", lambda e: e.activation(out=o_, in_=i_, func=AF.Copy), reads=[skey], writes=[bkey])
                else:
                    sc.op(en, lambda e: e.tensor_copy(out=o_, in_=i_), reads=[skey], writes=[bkey])

        def slab_store(sl, stb_t, bkey):
            nk, g, CW = sl["nk"], sl["g"], sl["CW"]
            dv = sl["dst"][sl["dch0"] + sl["c0"]:sl["dch0"] + sl["c0"] + g].rearrange("g p x -> p g x")
            sc.dma("pool", dv, stb_t[:, 0:g * nk * CW].rearrange("p (g x) -> p g x", g=g), reads=[bkey], writes=["pk"], key="k_" + bkey)

        def conv_specs(names, cap):
            L = []
            for nm in names:
                if nm == "l1_pool_w":
                    for g in range(4):
                        L += slab_list(Wd[nm], Pk[nm], 512, 512, 128, cap, row0=g * 512, dch0=g * 4)
                else:
                    K_, M_ = WSHAPES[nm]
                    L += slab_list(Wd[nm], Pk[nm], K_, M_, 192 if nm == "l0_b_w_uq" else 128, cap)
            return L

        EARLY = ["l0_ffn1_w_gate", "l0_ffn1_w_up", "l0_ffn1_w_down", "l0_w_in", "l0_b_w_uq", "l0_b_w_ukv"]
        LATE = [p + sfx for p in FFNS[1:] for sfx in ("_w_gate", "_w_up", "_w_down")] + ["l0_w_out", "l1_pool_w"]

        with ExitStack() as ph:
            stg = [ph.enter_context(nc.sbuf_tensor(f"stg{i}", [128, 8192], F32)) for i in range(2)]
            stb = [ph.enter_context(nc.sbuf_tensor(f"stb{i}", [128, 8192], BF16)) for i in range(2)]
            for n_, sl in enumerate(conv_specs(EARLY, 8192)):
                i = n_ % 2
                slab_load(sl, stg[i], f"stg{i}")
                slab_cast(sl, stg[i], f"stg{i}", stb[i], f"stb{i}", "act" if n_ % 2 else "dve")
                slab_store(sl, stb[i], f"stb{i}")
            sc.barrier()

        if stop_after < 1:
            sc.barrier()
            return nc

        def alloc_ffn(ph, tg):
            B = {}
            B["xres"] = ph.enter_context(nc.sbuf_tensor("xres" + tg, [128, DC, T], F32))
            B["xn"] = ph.enter_context(nc.sbuf_tensor("xn" + tg, [128, DC, T], BF16))
            B["hT"] = ph.enter_context(nc.sbuf_tensor("hT" + tg, [128, FC, T], BF16))
            B["y"] = ph.enter_context(nc.sbuf_tensor("ybuf" + tg, [128, DC, T], F32))
            B["wg"] = [ph.enter_context(nc.sbuf_tensor(f"wg{i}" + tg, [128, DC, 128], BF16)) for i in range(2)]
            B["wu"] = [ph.enter_context(nc.sbuf_tensor(f"wu{i}" + tg, [128, DC, 128], BF16)) for i in range(2)]
            B["wd"] = [ph.enter_context(nc.sbuf_tensor(f"wd{i}" + tg, [128, 6144], BF16)) for i in range(2)]
            B["sm"] = [ph.enter_context(nc.sbuf_tensor(f"sm{i}" + tg, [128, T], F32)) for i in range(16)]
            B["sqi"] = 0
            B["wgi"] = 0
            B["wdi"] = 0
            return B

        XR = [("xres", c) for c in range(DC)]
        XN = [("xn", c) for c in range(DC)]
        YK = [("y", c) for c in range(DC)]
        HK = [("hT", j) for j in range(FC)]

        def rstd_from_sumsq(B, ps_i, n, out_i, half=False, npart=128):
            sm = B["sm"]
            k = 4.0 if half else 1.0
            sc.op("dve", lambda e: e.tensor_scalar(out=sm[3][0:npart, :], in0=PS[ps_i][0:npart, :], scalar1=k / n, scalar2=k * EPS,
                                                   op0=ALU.mult, op1=ALU.add), reads=[PK[ps_i]], writes=["sm3"])
            sc.op("pool", lambda e: e.tensor_tensor(out=sm[out_i][0:npart, :], in0=sm[3][0:npart, :], in1=negh[0:npart, :], op=ALU.pow),
                  reads=["sm3", "negh"], writes=[f"sm{out_i}"])

        SQR = [0, 1, 14, 15]

        def sq_issue(B, src_ap, skeys, first, scale=None):
            sm = B["sm"]
            kw = {} if scale is None else {"scale": scale}
            if first:
                sc.op("act", lambda e: e.activation(out=sm[2][:], in_=src_ap, func=AF.Square, **kw), reads=skeys, writes=["sm2"])
                return None
            i = SQR[B["sqi"] % 4]
            B["sqi"] += 1
            sc.op("act", lambda e: e.activation(out=sm[i][:], in_=src_ap, func=AF.Square, **kw), reads=skeys, writes=[f"sm{i}"])
            return i

        def sq_add(B, i):
            sm = B["sm"]
            if i is None:
                return
            sc.op("dve", lambda e: e.tensor_tensor(out=sm[2][:], in0=sm[2][:], in1=sm[i][:], op=ALU.add), reads=[f"sm{i}", "sm2"], writes=["sm2"])

        def sq_acc(B, src_ap, skeys, first, scale=None):
            sq_add(B, sq_issue(B, src_ap, skeys, first, scale))

        def finish_stats(B, n, out_i, half=False, ps_i=7):
            sm = B["sm"]
            sc.op("pe", lambda e: e.matmul(PS[ps_i][:], lhsT=ONES, rhs=sm[2][:], start=True, stop=True), reads=["sm2", "cmat"], writes=[PK[ps_i]])
            rstd_from_sumsq(B, ps_i, n, out_i, half)

        def norm_stats(B, src, keys, nch, n, out_i, half=False, ps_i=7):
            for c in range(nch):
                sq_acc(B, src[:, c, :], [keys[c]], c == 0)
            finish_stats(B, n, out_i, half, ps_i)

        def apply_norm(B, src, skeys, gname, rs_i, dst, dkeys, nch=DC):
            sm = B["sm"]
            for c in range(nch):
                sc.op("dve", lambda e: e.scalar_tensor_tensor(out=dst[:, c, :], in0=src[:, c, :], scalar=gv(gname, c), in1=sm[rs_i][:],
                                                              op0=ALU.mult, op1=ALU.mult), reads=[skeys[c], f"sm{rs_i}", "cvec"], writes=[dkeys[c]])

        LAG = 3

        def ffn_prep_xn(B, p, c):
            xres, xn = B["xres"], B["xn"]
            sc.op("act", lambda e: e.activation(out=xn[:, c, :], in_=xres[:, c, :], func=AF.Identity, scale=gv(p + "_pre_g", c)),
                  reads=[XR[c], "cvec"], writes=[XN[c]])

        def tail(B, rs_i, next_p=None, stats=False):
            y, xres, sm = B["y"], B["xres"], B["sm"]
            want = next_p is not None or stats
            for c in range(DC):
                sc.op("dve", lambda e: e.tensor_tensor(out=y[:, c, :], in0=y[:, c, :], in1=sm[rs_i][:], op=ALU.mult),
                      reads=[YK[c], f"sm{rs_i}"], writes=[YK[c]])
                sc.op("dve", lambda e: e.tensor_tensor(out=xres[:, c, :], in0=xres[:, c, :], in1=y[:, c, :], op=ALU.add),
                      reads=[YK[c], XR[c]], writes=[XR[c]])
                if next_p is not None:
                    ffn_prep_xn(B, next_p, c)
                if want:
                    i = SQR[B["sqi"] % 4]
                    B["sqi"] += 1
                    sc.op("act", lambda e: e.activation(out=sm[i][:], in_=xres[:, c, :], func=AF.Square), reads=[XR[c]], writes=[f"sm{i}"])
                    sc.op("pe", lambda e: e.matmul(PS[7][:], lhsT=ONES, rhs=sm[i][:], start=(c == 0), stop=(c == DC - 1)),
                          reads=[f"sm{i}", "cmat"], writes=[PK[7]])
            if want:
                rstd_from_sumsq(B, 7, D, 5)

        def load_w(B, which, src_chunk):
            shp = list(src_chunk.shape)
            n = int(np.prod(shp[1:]))
            if which == "d":
                i = B["wdi"] % 2
                B["wdi"] += 1
                sc.dma("sp", B["wd"][i][:, 0:n], src_chunk, writes=[f"wd{i}"], key=f"k_wd{i}")
                return B["wd"][i], f"wd{i}"
            i = B["wgi"] % 4
            B["wgi"] += 1
            t = B["wg"][i // 2] if i % 2 == 0 else B["wu"][i // 2]
            k = f"wgu{i}"
            ov = t[:].rearrange("p k c -> p (k c)")[:, 0:n]
            if len(shp) == 3:
                ov = ov.rearrange("p (m x) -> p m x", m=shp[1])
            sc.dma("sp", ov, src_chunk, writes=[k], key="k_" + k)
            return t, k

        def ffn(B, p):
            xres, xn, hT, y, sm = B["xres"], B["xn"], B["hT"], B["y"], B["sm"]
            for j in range(FC):
                wg, kg = load_w(B, "g", Pk[p + "_w_gate"][j])
                wu, ku = load_w(B, "g", Pk[p + "_w_up"][j])
                pg, pu = (j % 2), 2 + (j % 2)
                for kc in range(DC):
                    sc.op("pe", lambda e: e.matmul(PS[pg][:], lhsT=wg[:, kc, :], rhs=xn[:, kc, :], start=(kc == 0), stop=(kc == DC - 1)),
                          reads=[kg, XN[kc]], writes=[PK[pg]], inc=(kc == DC - 1))
                for kc in range(DC):
                    sc.op("pe", lambda e: e.matmul(PS[pu][:], lhsT=wu[:, kc, :], rhs=xn[:, kc, :], start=(kc == 0), stop=(kc == DC - 1)),
                          reads=[ku, XN[kc]], writes=[PK[pu]], inc=(kc == DC - 1))
                a, b_, c_ = 8 + j % 2, 6 + j % 2, 10 + j % 2
                sc.op("dve", lambda e: e.tensor_tensor(out=sm[a][:], in0=PS[pg][:], in1=sm[5][:], op=ALU.mult), reads=[PK[pg], "sm5"], writes=[f"sm{a}"])
                sc.op("act", lambda e: e.activation(out=sm[b_][:], in_=sm[a][:], func=AF.Silu), reads=[f"sm{a}"], writes=[f"sm{b_}"])
                sc.op("pool", lambda e: e.tensor_tensor(out=sm[c_][:], in0=sm[b_][:], in1=sm[5][:], op=ALU.mult), reads=[f"sm{b_}", "sm5"], writes=[f"sm{c_}"])
                sc.op("dve", lambda e: e.tensor_tensor(out=hT[:, j, :], in0=sm[c_][:], in1=PS[pu][:], op=ALU.mult),
                      reads=[f"sm{c_}", PK[pu]], writes=[HK[j]])
            for m in range(DC):
                wd, kd = load_w(B, "d", Pk[p + "_w_down"][m])
                pd = 4 + (m % 2)
                for j in range(FC):
                    sc.op("pe", lambda e: e.matmul(PS[pd][:], lhsT=wd[:, j * 128:(j + 1) * 128], rhs=hT[:, j, :], start=(j == 0), stop=(j == FC - 1)),
                          reads=[kd, HK[j]], writes=[PK[pd]], inc=(j == FC - 1))
                sc.op("act", lambda e: e.activation(out=y[:, m, :], in_=PS[pd][:], func=AF.Identity, scale=gv(p + "_post_g", m)),
                      reads=[PK[pd], "cvec"], writes=[YK[m]])
                sq_acc(B, PS[pd][:], [PK[pd]], m == 0)
            finish_stats(B, D, 12, half=True)

        def load_x_tokmajor(B, t):
            y, xres = B["y"], B["xres"]
            for s4 in range(4):
                stv = y[:, 4 * s4:4 * s4 + 4, :].rearrange("p c t -> p (c t)")
                sc.dma("pool", stv, x_in[t * T + s4 * 128: t * T + (s4 + 1) * 128, :], writes=YK[4 * s4:4 * s4 + 4], key=f"k_xs{s4}")
            slots = {}
            for c in range(DC + LAG):
                if c < DC:
                    pi = c % 2
                    for s4 in range(4):
                        stv = y[:, 4 * s4:4 * s4 + 4, :].rearrange("p c t -> p (c t)")
                        sc.op("pe", lambda e: e.transpose(out=PS[pi][:, s4 * 128:(s4 + 1) * 128], in_=stv[:, c * 128:(c + 1) * 128], identity=IDENT),
                              reads=YK[4 * s4:4 * s4 + 4] + ["cmat"], writes=[PK[pi]], inc=(s4 == 3))
                    sc.op("dve", lambda e: e.tensor_copy(out=xres[:, c, :], in_=PS[pi][:]), reads=[PK[pi]], writes=[XR[c]])
                    ffn_prep_xn(B, "l0_ffn1", c)
                    slots[c] = sq_issue(B, xres[:, c, :], [XR[c]], c == 0)
                if c - LAG >= 0:
                    sq_add(B, slots[c - LAG])
            finish_stats(B, D, 5)

        def store_fm(B, src, keys, dst, t, tag):
            sc.dma("pool", dst[:, :, t * T:(t + 1) * T].rearrange("c p t -> p c t"), src[:], reads=keys, writes=[tag], key="k_st_" + tag)

        def load_fm(B, dstt, keys, src, t, tag):
            sc.dma("pool", dstt[:], src[:, :, t * T:(t + 1) * T].rearrange("c p t -> p c t"), reads=[tag], writes=keys, key="k_ld_" + tag)

        with ExitStack() as ph:
            B = alloc_ffn(ph, "_a")
            sm, y, hT, xn, xres = B["sm"], B["y"], B["hT"], B["xn"], B["xres"]
            Win = Pk["l0_w_in"]
            for t in range(NT):
                tsl = slice(t * T, (t + 1) * T)
                load_x_tokmajor(B, t)
                ffn(B, "l0_ffn1")
                tail(B, 12, stats=True)
                store_fm(B, xres, XR, X1, t, "X1")
                if stop_after < 2:
                    continue
                sc.dma("pool", sm[8][:], ropeA_d[0, :, tsl], writes=["sm8"], key="k_r0")
                sc.dma("pool", sm[9][:], ropeA_d[1, :, tsl], writes=["sm9"], key="k_r1")
                sc.dma("pool", sm[10][0:64, :], ropeB_d[0, :, tsl], writes=["sm10"], key="k_r2")
                sc.dma("pool", sm[11][0:64, :], ropeB_d[1, :, tsl], writes=["sm11"], key="k_r3")
                B["wdi"] = 0
                sc.dma("sp", B["wd"][0][:, 0:8 * 768].rearrange("p (h x) -> p h x", h=8), Pk["l0_b_w_uq"].rearrange("h p x -> p h x"),
                       writes=["wd0"], key="k_wd0")
                sc.dma("sp", B["wd"][1][:, 0:16 * 256].rearrange("p (h x) -> p h x", h=16), Pk["l0_b_w_ukv"].rearrange("h p x -> p h x"),
                       writes=["wd1"], key="k_wd1")
                wuq = B["wd"][0][:, 0:8 * 768].rearrange("p (h k c) -> p h k c", h=8, k=4)
                wukv = B["wd"][1][:, 0:16 * 256].rearrange("p (m k c) -> p m k c", m=16, k=2)
                apply_norm(B, xres, XR, "l0_mix_pre_g", 5, xn, XN)

                def proj(mc, ps_i, M=128):
                    w, kw = load_w(B, "g", Win[mc])
                    for kc in range(DC):
                        sc.op("pe", lambda e: e.matmul(PS[ps_i][0:M, :], lhsT=w[:, kc, 0:M], rhs=xn[:, kc, :], start=(kc == 0), stop=(kc == DC - 1)),
                              reads=[kw, XN[kc]], writes=[PK[ps_i]], inc=(kc == DC - 1))

                def rope(src_ap, skey, np_, perm, ci, si, dst_ap, dkey, ytmp):
                    sc.op("pe", lambda e: e.matmul(PS[6][0:np_, :], lhsT=perm, rhs=src_ap, start=True, stop=True), reads=[skey, "cmat"], writes=[PK[6]])
                    sc.op("pool", lambda e: e.tensor_tensor(out=y[0:np_, ytmp, :], in0=src_ap, in1=sm[ci][0:np_, :], op=ALU.mult),
                          reads=[skey, f"sm{ci}"], writes=[YK[ytmp]])
                    sc.op("dve", lambda e: e.tensor_tensor(out=y[0:np_, ytmp + 1, :], in0=PS[6][0:np_, :], in1=sm[si][0:np_, :], op=ALU.mult),
                          reads=[PK[6], f"sm{si}"], writes=[YK[ytmp + 1]])
                    sc.op("dve", lambda e: e.tensor_tensor(out=dst_ap, in0=y[0:np_, ytmp, :], in1=y[0:np_, ytmp + 1, :], op=ALU.add),
                          reads=[YK[ytmp], YK[ytmp + 1]], writes=[dkey])

                def a_stage0(mc):
                    pi = mc % 3
                    yr = 4 * (mc % 3)
                    proj(mc, pi)
                    sc.op("act", lambda e: e.activation(out=y[:, yr, :], in_=PS[pi][:], func=AF.Copy), reads=[PK[pi]], writes=[YK[yr]])
                    sq = 2 if mc % 2 == 0 else 13
                    sc.op("act", lambda e: e.activation(out=sm[sq][:], in_=PS[pi][:], func=AF.Square), reads=[PK[pi]], writes=[f"sm{sq}"])

                def a_stage1(mc):
                    yr = 4 * (mc % 3)
                    sq = 2 if mc % 2 == 0 else 13
                    sc.op("pe", lambda e: e.matmul(PS[7][:], lhsT=ONES, rhs=sm[sq][:], start=True, stop=True), reads=[f"sm{sq}", "cmat"], writes=[PK[7]])
                    rstd_from_sumsq(B, 7, 128, 5)
                    gname = "l0_a_q_norm_g" if mc < 8 else "l0_a_k_norm_g"
                    sc.op("dve", lambda e: e.scalar_tensor_tensor(out=y[:, yr + 1, :], in0=y[:, yr, :], scalar=gv(gname, 0), in1=sm[5][:],
                                                                  op0=ALU.mult, op1=ALU.mult), reads=[YK[yr], "sm5", "cvec"], writes=[YK[yr + 1]])

                def a_stage2(mc):
                    yr = 4 * (mc % 3)
                    ob = hT[:, mc, :]
                    rope(y[:, yr + 1, :], YK[yr + 1], 128, PERMA, 8, 9, ob, HK[mc], yr + 2)
                    dst = QA[mc, :, tsl] if mc < 8 else KA[mc - 8, :, tsl]
                    sc.dma("pool", dst, ob, reads=[HK[mc]], writes=["QKA"], key=f"k_o{mc % 4}")

                for i in range(12):
                    if i < 10:
                        a_stage0(i)
                    if 0 <= i - 1 < 10:
                        a_stage1(i - 1)
                    if 0 <= i - 2 < 10:
                        a_stage2(i - 2)
                def lowrank(mcs, gname, n, ybase, hbase):
                    nch = len(mcs)
                    for i, mc in enumerate(mcs):
                        pi = i % 2
                        proj(mc, pi)
                        sc.op("act", lambda e: e.activation(out=y[:, ybase + i, :], in_=PS[pi][:], func=AF.Copy), reads=[PK[pi]], writes=[YK[ybase + i]])
                    norm_stats(B, y[:, ybase:ybase + nch, :], YK[ybase:ybase + nch], nch, n, 5)
                    for i in range(nch):
                        sc.op("dve", lambda e: e.scalar_tensor_tensor(out=hT[:, hbase + i, :], in0=y[:, ybase + i, :], scalar=gv(gname, i), in1=sm[5][:],
                                                                      op0=ALU.mult, op1=ALU.mult), reads=[YK[ybase + i], "sm5", "cvec"], writes=[HK[hbase + i]])

                lowrank([12, 13, 14, 15], "l0_b_cq_norm_g", 512, 8, 12)
                for vi in range(2):
                    w, kw = load_w(B, "g", Win[10 + vi])
                    for s4 in range(4):
                        for kc in range(DC):
                            sc.op("pe", lambda e: e.matmul(PS[vi][:, s4 * 128:(s4 + 1) * 128], lhsT=xn[:, kc, s4 * 128:(s4 + 1) * 128], rhs=w[:, kc, :],
                                                           start=(kc == 0), stop=(kc == DC - 1)), reads=[kw, XN[kc]], writes=[PK[vi]],
                                  inc=(kc == DC - 1 and s4 == 3))
                    ob = hT[:, 10 + vi, :]
                    sc.op("act", lambda e: e.activation(out=ob, in_=PS[vi][:], func=AF.Copy), reads=[PK[vi]], writes=[HK[10 + vi]])
                    sc.dma("pool", VA[tsl, vi * 128:(vi + 1) * 128].rearrange("(s p) d -> p s d", p=128),
                           ob.rearrange("p (s d) -> p s d", s=4), reads=[HK[10 + vi]], writes=["VA"], key=f"k_o{vi}")

                lowrank([16, 17], "l0_b_ckv_norm_g", 256, 12, 16)
                proj(18, 4, M=64)
                sc.op("act", lambda e: e.activation(out=y[0:64, 0, :], in_=PS[4][0:64, :], func=AF.Copy), reads=[PK[4]], writes=[YK[0]])
                ob2 = hT[0:64, 36, :]
                rope(y[0:64, 0, :], YK[0], 64, PERMB, 10, 11, ob2, HK[36], 1)
                sc.dma("pool", KBr[:, tsl], ob2, reads=[HK[36]], writes=["KBr"], key="k_kr")
                for h in range(8):
                    pi = h % 2
                    for kc in range(4):
                        sc.op("pe", lambda e: e.matmul(PS[pi][:], lhsT=wuq[:, h, kc, 0:128], rhs=hT[:, 12 + kc, :], start=(kc == 0), stop=(kc == 3)),
                              reads=["wd0", HK[12 + kc]], writes=[PK[pi]], inc=(kc == 3))
                    ob = hT[:, 20 + (h % 4), :]
                    sc.op("act", lambda e: e.activation(out=ob, in_=PS[pi][:], func=AF.Copy), reads=[PK[pi]], writes=[HK[20 + h % 4]])
                    sc.dma("pool", QBn[h, :, tsl], ob, reads=[HK[20 + h % 4]], writes=["QBn"], key=f"k_o{h % 4}")
                    pj = 2 + h % 2
                    for kc in range(4):
                        sc.op("pe", lambda e: e.matmul(PS[pj][0:64, :], lhsT=wuq[:, h, kc, 128:192], rhs=hT[:, 12 + kc, :], start=(kc == 0), stop=(kc == 3)),
                              reads=["wd0", HK[12 + kc]], writes=[PK[pj]], inc=(kc == 3))
                    yr = 4 * (h % 2)
                    sc.op("act", lambda e: e.activation(out=y[0:64, yr, :], in_=PS[pj][0:64, :], func=AF.Copy), reads=[PK[pj]], writes=[YK[yr]])
                    ob2 = hT[0:64, 24 + (h % 4), :]
                    rope(y[0:64, yr, :], YK[yr], 64, PERMB, 10, 11, ob2, HK[24 + h % 4], yr + 1)
                    sc.dma("pool", QBr[h, :, tsl], ob2, reads=[HK[24 + h % 4]], writes=["QBr"], key=f"k_p{h % 4}")
                for h in range(8):
                    pi = h % 2
                    for kc in range(2):
                        sc.op("pe", lambda e: e.matmul(PS[pi][:], lhsT=wukv[:, 2 * h, kc, :], rhs=hT[:, 16 + kc, :], start=(kc == 0), stop=(kc == 1)),
                              reads=["wd1", HK[16 + kc]], writes=[PK[pi]], inc=(kc == 1))
                    ob = hT[:, 28 + (h % 4), :]
                    sc.op("act", lambda e: e.activation(out=ob, in_=PS[pi][:], func=AF.Copy), reads=[PK[pi]], writes=[HK[28 + h % 4]])
                    sc.dma("pool", KBn[h, :, tsl], ob, reads=[HK[28 + h % 4]], writes=["KBn"], key=f"k_q{h % 4}")
                for s4 in range(4):
                    for hh in range(2):
                        pi = 2 + hh
                        for kc in range(2):
                            sc.op("pe", lambda e: e.matmul(PS[pi][:].rearrange("p (h c) -> p h c", h=4), lhsT=hT[:, 16 + kc, s4 * 128:(s4 + 1) * 128],
                                                           rhs=wukv[:, 8 * hh + 1:8 * hh + 8:2, kc, :], start=(kc == 0), stop=(kc == 1)),
                                  reads=["wd1", HK[16 + kc]], writes=[PK[pi]], inc=(kc == 1))
                        ob = hT[:, 32 + 2 * (s4 % 2) + hh, :]
                        hk = HK[32 + 2 * (s4 % 2) + hh]
                        sc.op("act", lambda e: e.activation(out=ob, in_=PS[pi][:], func=AF.Copy), reads=[PK[pi]], writes=[hk])
                        sc.dma("pool", VB[t * T + s4 * 128:t * T + (s4 + 1) * 128, hh * 512:(hh + 1) * 512], ob, reads=[hk], writes=["VB"],
                               key=f"k_v{(2 * s4 + hh) % 4}")
            sc.barrier()

        if stop_after < 3:
            sc.barrier()
            return nc

        with ExitStack() as ph:
            KT = [ph.enter_context(nc.sbuf_tensor(f"KT{i}", [128, S], BF16)) for i in range(2)]
            VT = [ph.enter_context(nc.sbuf_tensor(f"VT{i}", [128, NKC, 128], BF16)) for i in range(2)]
            KR = ph.enter_context(nc.sbuf_tensor("KR", [128, S], BF16))
            QT = [ph.enter_context(nc.sbuf_tensor(f"QT{i}", [128, T], BF16)) for i in range(3)]
            QR = [ph.enter_context(nc.sbuf_tensor(f"QR{i}", [128, T], BF16)) for i in range(3)]
            PT = [ph.enter_context(nc.sbuf_tensor(f"PT{i}", [128, T], BF16)) for i in range(8)]
            RI = [ph.enter_context(nc.sbuf_tensor(f"RI{i}", [128, T], F32)) for i in range(2)]
            OB = [ph.enter_context(nc.sbuf_tensor(f"OB{i}", [128, T], BF16)) for i in range(2)]
            ACC = [ph.enter_context(nc.sbuf_tensor(f"ACC{i}", [128, T], F32)) for i in range(8)]
            sc.op("pool", lambda e: e.memset(KR[64:128, :], 0.0), writes=["KR"])
            for i in range(3):
                sc.op("pool", lambda e: e.memset(QR[i][64:128, :], 0.0), writes=[f"QR{i}"])
            sc.dma("pool", KR[0:64, :], KBr, reads=["KBr"], writes=["KR"], key="k_KR")
            NQT = S // T
            items = [(hd, qt) for hd in range(16) for qt in range(NQT)]
            LCAP = 5632
            stgL = [ph.enter_context(nc.sbuf_tensor(f"stgL{i}", [128, LCAP], F32)) for i in range(2)]
            stbL = [ph.enter_context(nc.sbuf_tensor(f"stbL{i}", [128, LCAP], BF16)) for i in range(2)]
            late = conv_specs(LATE, LCAP)
            per_item = (len(late) + len(items) - 1) // len(items)
            lpos = [0, 0]

            def late_step(k):
                for _ in range(k):
                    if lpos[1] < lpos[0]:
                        j = lpos[1]
                        i = j % 2
                        slab_cast(late[j], stgL[i], f"stgL{i}", stbL[i], f"stbL{i}", "pool", pieces=4)
                        slab_store(late[j], stbL[i], f"stbL{i}")
                        lpos[1] += 1
                    if lpos[0] < len(late) and lpos[0] - lpos[1] < 2:
                        j = lpos[0]
                        i = j % 2
                        slab_load(late[j], stgL[i], f"stgL{i}")
                        lpos[0] += 1

            def head_loads(hd):
                h, b = hd % 8, hd % 2
                if hd < 8:
                    ksrc = KA[h // 4]
                    vsrc = VA[:, (h // 4) * 128:(h // 4 + 1) * 128].rearrange("(c p) d -> p c d", p=128)
                else:
                    ksrc = KBn[h]
                    vsrc = VB[:, h * 128:(h + 1) * 128].rearrange("(c p) d -> p c d", p=128)
                sc.dma("sp", KT[b][:], ksrc, writes=[f"KT{b}"], key=f"k_KT{b}")
                for v4 in range(0, NKC, 16):
                    v5 = min(NKC, v4 + 16)
                    sc.dma("sp", VT[b][:, v4:v5, :], vsrc[:, v4:v5, :], writes=[f"VT{b}"], key=f"k_VT{b}")

            def q_load(n):
                hd, qt = items[n]
                h, q3 = hd % 8, n % 3
                qsl = slice(qt * T, (qt + 1) * T)
                sc.dma("pool", QT[q3][:], (QBn if hd >= 8 else QA)[h, :, qsl], writes=[f"QT{q3}"], key=f"k_QT{q3}")
                if hd >= 8:
                    sc.dma("pool", QR[q3][0:64, :], QBr[h, :, qsl], writes=[f"QR{q3}"], key=f"k_QR{q3}")

            head_loads(0)
            q_load(0)
            pending = [None]
            for n, (hd, qt) in enumerate(items):
                isB = hd >= 8
                h, b = hd % 8, hd % 2
                scale = (192.0 if isB else 128.0) ** -0.5
                q3, o2 = n % 3, n % 2
                qsl = slice(qt * T, (qt + 1) * T)
                if qt == 0 and hd + 1 < 16:
                    head_loads(hd + 1)
                if n + 1 < len(items):
                    q_load(n + 1)
                late_step(per_item)
                pO, pL = 4 + o2, 6 + o2
                NSB, LA = 4, 3
                st = {"d": 0, "l": 0}

                def s_mm(kc):
                    pi = kc % NSB
                    pt = kc % 8
                    ksl = slice(kc * 128, (kc + 1) * 128)
                    if isB:
                        sc.op("pe", lambda e: e.matmul(PS[pi][:], lhsT=KT[b][:, ksl], rhs=QT[q3][:], start=True, stop=False),
                              reads=[f"KT{b}", f"QT{q3}"], writes=[PK[pi]], inc=False)
                        sc.op("pe", lambda e: e.matmul(PS[pi][:], lhsT=KR[:, ksl], rhs=QR[q3][:], start=False, stop=True),
                              reads=["KR", f"QR{q3}"], writes=[PK[pi]])
                    else:
                        sc.op("pe", lambda e: e.matmul(PS[pi][:], lhsT=KT[b][:, ksl], rhs=QT[q3][:], start=True, stop=True),
                              reads=[f"KT{b}", f"QT{q3}"], writes=[PK[pi]])
                    sc.op("act", lambda e: e.activation(out=PT[pt][:], in_=PS[pi][:], func=AF.Exp, scale=scale), reads=[PK[pi]], writes=[f"PT{pt}"])

                def pv_mm(kc):
                    pt = kc % 8
                    sc.op("pe", lambda e: e.matmul(PS[pO][:], lhsT=VT[b][:, kc, :], rhs=PT[pt][:], start=(kc == 0), stop=(kc == NKC - 1)),
                          reads=[f"VT{b}", f"PT{pt}"], writes=[PK[pO]], inc=True)
                    if (not isB) and kc % 3 == 2:
                        sc.op("pe", lambda e: e.matmul(PS[pL][:], lhsT=onesb[:], rhs=PT[pt][:], start=(st["l"] == 0), stop=False),
                              reads=["onesb", f"PT{pt}"], writes=[PK[pL]], inc=True)
                        st["l"] += 1
                        return
                    a = 4 * o2 + st["d"] % 4
                    if st["d"] < 4:
                        sc.op("dve", lambda e: e.tensor_copy(out=ACC[a][:], in_=PT[pt][:]), reads=[f"PT{pt}"], writes=[f"ACC{a}"])
                    else:
                        sc.op("dve", lambda e: e.tensor_tensor(out=ACC[a][:], in0=ACC[a][:], in1=PT[pt][:], op=ALU.add),
                              reads=[f"PT{pt}", f"ACC{a}"], writes=[f"ACC{a}"])
                    st["d"] += 1

                def make_epi(hd=hd, qsl=qsl, o2=o2, pO=pO, pL=pL, st=st):
                    def epi():
                        na = min(4, st["d"])
                        for a4 in range(na):
                            a = 4 * o2 + a4
                            sc.op("pe", lambda e: e.matmul(PS[pL][:], lhsT=ONES, rhs=ACC[a][:], start=(a4 == 0 and st["l"] == 0), stop=(a4 == na - 1)),
                                  reads=["cmat", f"ACC{a}"], writes=[PK[pL]], inc=(a4 == na - 1))
                        sc.op("dve", lambda e: e.reciprocal(out=RI[o2][:], in_=PS[pL][:]), reads=[PK[pL]], writes=[f"RI{o2}"])
                        sc.op("dve", lambda e: e.tensor_tensor(out=OB[o2][:], in0=PS[pO][:], in1=RI[o2][:], op=ALU.mult),
                              reads=[PK[pO], f"RI{o2}"], writes=[f"OB{o2}"])
                        sc.dma("pool", ATT[hd, :, qsl], OB[o2][:], reads=[f"OB{o2}"], writes=["ATT"], key=f"k_OB{o2}")
                    return epi

                for kc in range(min(LA, NKC)):
                    s_mm(kc)
                for kc in range(NKC):
                    if kc + LA < NKC:
                        s_mm(kc + LA)
                    pv_mm(kc)
                    if kc == min(5, NKC - 2) and pending[0] is not None:
                        pending[0]()
                        pending[0] = None
                if pending[0] is not None:
                    pending[0]()
                pending[0] = make_epi()
            pending[0]()
            while lpos[1] < len(late):
                late_step(1)
            sc.barrier()

        if stop_after < 4:
            sc.barrier()
            return nc

        with ExitStack() as ph:
            B = alloc_ffn(ph, "_b")
            sm, y, hT, xn, xres = B["sm"], B["y"], B["hT"], B["xn"], B["xres"]
            for t in range(NT):
                load_fm(B, xres, XR, X1, t, "X1")
                load_fm(B, xn, XN, ATT, t, "ATT")
                for m in range(DC):
                    w, kw = load_w(B, "g", Pk["l0_w_out"][m])
                    pi = m % 2
                    for kc in range(DC):
                        sc.op("pe", lambda e: e.matmul(PS[pi][:], lhsT=w[:, kc, :], rhs=xn[:, kc, :], start=(kc == 0), stop=(kc == DC - 1)),
                              reads=[kw, XN[kc]], writes=[PK[pi]], inc=(kc == DC - 1))
                    sc.op("act", lambda e: e.activation(out=y[:, m, :], in_=PS[pi][:], func=AF.Identity, scale=gv("l0_mix_post_g", m)),
                          reads=[PK[pi], "cvec"], writes=[YK[m]])
                    sq_acc(B, PS[pi][:], [PK[pi]], m == 0)
                finish_stats(B, D, 12)
                tail(B, 12, next_p="l0_ffn2")
                ffn(B, "l0_ffn2")
                tail(B, 12, next_p="l1_ffn1")
                ffn(B, "l1_ffn1")
                tail(B, 12, stats=True)
                store_fm(B, xres, XR, X4, t, "X4")
                for c in range(DC):
                    sc.op("dve", lambda e: e.scalar_tensor_tensor(out=y[:, c, :], in0=xres[:, c, :], scalar=gv("l1_mix_pre_g", c), in1=sm[5][:],
                                                                  op0=ALU.mult, op1=ALU.mult), reads=[XR[c], "sm5", "cvec"], writes=[YK[c]])
                store_fm(B, y, YK, HN4, t, "HN4")
            sc.barrier()

            if stop_after < 5:
                sc.barrier()
                return nc

            HW = T + 16
            hf = hT[:].rearrange("p c t -> p (c t)").bitcast(F32)
            REG = [hf[:, r * 2560:r * 2560 + 4 * HW].rearrange("p (c t) -> p c t", c=4) for r in range(3)]
            RK = [HK[10 * r:10 * r + 9] for r in range(3)]
            TMP = hf[:, 7680:7680 + 4 * T].rearrange("p (c t) -> p c t", c=4)
            TK4 = [HK[30 + 2 * c4:32 + 2 * c4] for c4 in range(4)]

            def pooling(t):
                for w4 in range(4):
                    sc.dma("pool", sm[8 + w4][:], invc_d[w4, :, t * T:(t + 1) * T], writes=[f"sm{8 + w4}"], key=f"k_r{w4}")
                for g in range(4):
                    wwin = 2 << g
                    hw = wwin // 2
                    R0, R1, R2 = REG
                    lo = t * T - 8
                    hi = (t + 1) * T + 8
                    clo, chi = max(lo, 0), min(hi, S)
                    if clo > lo:
                        sc.op("pool", lambda e: e.memset(R0[:, :, 0:clo - lo], 0.0), writes=RK[0])
                    if chi < hi:
                        sc.op("pool", lambda e: e.memset(R0[:, :, HW - (hi - chi):HW], 0.0), writes=RK[0])
                    sc.dma("pool", R0[:, :, clo - lo:chi - lo], HN4[4 * g:4 * g + 4, :, clo:chi].rearrange("c p t -> p c t"),
                           reads=["HN4"], writes=RK[0], key="k_halo")
                    cur, ck = R0, RK[0]
                    n = HW
                    step = 1
                    lvl = 0
                    while step < wwin:
                        dstR, dk = (R1, RK[1]) if lvl % 2 == 0 else (R2, RK[2])
                        n2 = n - step
                        sc.op("dve", lambda e: e.tensor_tensor(out=dstR[:, :, 0:n2], in0=cur[:, :, 0:n2], in1=cur[:, :, step:step + n2], op=ALU.add),
                              reads=ck, writes=dk)
                        cur, ck, n = dstR, dk, n2
                        step *= 2
                        lvl += 1
                    for c4 in range(4):
                        c = 4 * g + c4
                        sc.op("dve", lambda e: e.tensor_tensor(out=TMP[:, c4, :], in0=cur[:, c4, 8 - hw:8 - hw + T], in1=sm[8 + g][:], op=ALU.mult),
                              reads=ck + [f"sm{8 + g}"], writes=TK4[c4])
                        sc.op("pool", lambda e: e.tensor_tensor(out=xn[:, c, :], in0=TMP[:, c4, :], in1=R0[:, c4, 8:8 + T], op=ALU.subtract),
                              reads=TK4[c4] + RK[0], writes=[XN[c]])

            pooling(0)
            for t in range(NT):
                for g in range(4):
                    w, kw = load_w(B, "g", Pk["l1_pool_w"][4 * g:4 * g + 4].rearrange("m p x -> p m x"))
                    wv = w[:].rearrange("p k c -> p (k c)")[:, 0:2048].rearrange("p (m k c) -> p m k c", m=4, k=4)
                    for m4 in range(4):
                        m = 4 * g + m4
                        pi = m % 2
                        for kc in range(4):
                            sc.op("pe", lambda e: e.matmul(PS[pi][:], lhsT=wv[:, m4, kc, :], rhs=xn[:, 4 * g + kc, :], start=(kc == 0), stop=(kc == 3)),
                                  reads=[kw, XN[4 * g + kc]], writes=[PK[pi]], inc=(kc == 3))
                        sc.op("act", lambda e: e.activation(out=y[:, m, :], in_=PS[pi][:], func=AF.Identity, scale=pg2[:, m:m + 1]),
                              reads=[PK[pi], "pg2"], writes=[YK[m]])
                        sq_acc(B, PS[pi][:], [PK[pi]], m == 0, scale=gv("l1_pool_scale", m))
                load_fm(B, xres, XR, X4, t, "X4")
                finish_stats(B, D, 12)
                tail(B, 12, next_p="l1_ffn2")
                ffn(B, "l1_ffn2")
                tail(B, 12)
                if t + 1 < NT:
                    pooling(t + 1)
                for s4 in range(4):
                    stv = y[:, 4 * s4:4 * s4 + 4, :].rearrange("p c t -> p (c t)")
                    for c4 in range(4):
                        pi = c4 % 2
                        for cc in range(4):
                            c = 4 * c4 + cc
                            sc.op("pe", lambda e: e.transpose(out=PS[pi][:, cc * 128:(cc + 1) * 128], in_=xres[:, c, s4 * 128:(s4 + 1) * 128], identity=IDENT),
                                  reads=[XR[c], "cmat"], writes=[PK[pi]], inc=(cc == 3))
                        sc.op("act", lambda e: e.activation(out=stv[:, c4 * 512:(c4 + 1) * 512], in_=PS[pi][:], func=AF.Copy), reads=[PK[pi]], writes=[YK[4 * s4 + c4]])
                    sc.dma("pool", y_out[t * T + s4 * 128:t * T + (s4 + 1) * 128, :], stv, reads=YK[4 * s4:4 * s4 + 4], writes=["yout"], key=f"k_ys{s4}")
            sc.barrier()
        sc.barrier()
    return nc


def make_consts(S):
    cmat = np.zeros((128, 4, 128), np.float32)
    cmat[:, 0, :] = 1.0
    cmat[:, 1, :] = np.eye(128, dtype=np.float32)
    for i in range(64):
        cmat[2 * i + 1, 2, 2 * i] = -1.0
        cmat[2 * i, 2, 2 * i + 1] = 1.0
    for i in range(32):
        cmat[2 * i + 1, 3, 2 * i] = -1.0
        cmat[2 * i, 3, 2 * i + 1] = 1.0
    tt = np.arange(S)
    row = (tt // GRID_W).astype(np.float32)
    col = (tt % GRID_W).astype(np.float32)

    def tab(rot_dim):
        half = rot_dim // 2
        freqs = (np.float32(10000.0) ** (-np.arange(0, half, 2, dtype=np.float32) / np.float32(half))).astype(np.float32)
        ang = np.concatenate([row[:, None] * freqs, col[:, None] * freqs], axis=-1).astype(np.float32)
        c = np.repeat(np.cos(ang).T, 2, axis=0)
        s = np.repeat(np.sin(ang).T, 2, axis=0)
        return np.ascontiguousarray(np.stack([c, s]).astype(np.float32))

    ropeA = tab(128)
    ropeB = tab(64)
    invc = np.zeros((4, 128, S), np.float32)
    for g, w in enumerate((2, 4, 8, 16)):
        lo = np.clip(tt - w // 2, 0, S)
        hi = np.clip(tt + w // 2, 0, S)
        invc[g, :, :] = (1.0 / (hi - lo).astype(np.float32))[None, :]
    return cmat, ropeA, ropeB, invc


def make_inputs(weights, S):
    cvec = np.zeros((128, NCV), np.float32)
    for n, l in VECS:
        cvec[:, VCOL[n]:VCOL[n] + l // 128] = np.asarray(weights[n], np.float32).reshape(l // 128, 128).T
    cmat, ropeA, ropeB, invc = make_consts(S)
    base = {"cvec": cvec, "cmat": cmat, "ropeA": ropeA, "ropeB": ropeB, "invc": invc}
    for n, s in WSHAPES.items():
        base[n] = np.ascontiguousarray(np.asarray(weights[n], np.float32).reshape(s))
    return base


_NC_CACHE = {}


def run_seqs(seqs, weights, S, n_cores=8, **bk):
    key = (S, tuple(sorted(bk.items())))
    if key not in _NC_CACHE:
        _NC_CACHE[key] = build(S, **bk)
    nc = _NC_CACHE[key]
    base = make_inputs(weights, S)
    in_maps = []
    for i in range(n_cores):
        m = dict(base)
        m["x"] = np.ascontiguousarray(seqs[i % len(seqs)], dtype=np.float32)
        in_maps.append(m)
    res = run_bass_kernel_spmd(nc, in_maps, core_ids=list(range(n_cores)))
    return res


def kernel(**inputs):
    xp = np.asarray(inputs["x_prompt"], np.float32)
    xs = np.asarray(inputs["x_sample"], np.float32)
    S = xp.shape[1]
    seqs = [xp[0], xp[1], xs[0], xs[1], xs[2], xs[3]]
    res = run_seqs(seqs, inputs, S)
    outs = [res.results[i]["y"] for i in range(6)]
    return (np.stack(outs[0:2]).astype(np.float32), np.stack(outs[2:6]).astype(np.float32))
```

```python
import numpy as np
from contextlib import ExitStack
import concourse.bass as bass
import concourse.mybir as mybir
from concourse.bass_utils import run_bass_kernel_spmd

F32 = mybir.dt.float32
BF16 = mybir.dt.bfloat16
AF = mybir.ActivationFunctionType
ALU = mybir.AluOpType

T = 512
D = 2048
DC = 16
DFF = 5632
FC = 44
EPS = 1e-6
GRID_W = 64
IN_COLS = 2368
FFNS = ["l0_ffn1", "l0_ffn2", "l1_ffn1", "l1_ffn2"]

WSHAPES = {}
for _p in FFNS:
    WSHAPES[_p + "_w_gate"] = (D, DFF)
    WSHAPES[_p + "_w_up"] = (D, DFF)
    WSHAPES[_p + "_w_down"] = (DFF, D)
WSHAPES["l0_w_in"] = (D, IN_COLS)
WSHAPES["l0_b_w_uq"] = (512, 1536)
WSHAPES["l0_b_w_ukv"] = (256, 2048)
WSHAPES["l0_w_out"] = (D, D)
WSHAPES["l1_pool_w"] = (4 * 512, 512)

VECS = []
for _p in FFNS:
    VECS += [(_p + "_pre_g", 2048), (_p + "_post_g", 2048)]
VECS += [("l0_mix_pre_g", 2048), ("l0_mix_post_g", 2048), ("l1_mix_pre_g", 2048),
         ("l1_mix_post_g", 2048), ("l1_pool_scale", 2048), ("l0_a_q_norm_g", 128),
         ("l0_a_k_norm_g", 128), ("l0_b_cq_norm_g", 512), ("l0_b_ckv_norm_g", 256)]
VCOL = {}
_c = 0
for _n, _l in VECS:
    VCOL[_n] = _c
    _c += _l // 128
NCV = _c


class Sched:
    def __init__(self, nc, es):
        self.nc, self.es = nc, es
        self.sems = {}
        self.eng = {}
        for name, e in [("pe", nc.tensor), ("act", nc.scalar), ("dve", nc.vector),
                        ("pool", nc.gpsimd), ("sp", nc.sync)]:
            self.sems[name] = es.enter_context(nc.semaphore("sem_" + name))
            self.eng[name] = dict(e=e, cnt=0, seen={})
        self.lastw = {}
        self.readers = {}
        self.dcnt = {}
        self.n = 0

    def _waits(self, en, reads, writes):
        E = self.eng[en]
        need = {}
        for k in reads:
            t = self.lastw.get(k)
            if t is not None and need.get(t[0], 0) < t[1]:
                need[t[0]] = t[1]
        for k in writes:
            t = self.lastw.get(k)
            if t is not None and need.get(t[0], 0) < t[1]:
                need[t[0]] = t[1]
            r = self.readers.get(k)
            if r:
                for s, v in r.items():
                    if need.get(s, 0) < v:
                        need[s] = v
        for s, v in need.items():
            if s == "pe" and en == "pe":
                continue
            if E["seen"].get(s, 0) < v:
                E["e"].wait_ge(self.sems[s], v)
                E["seen"][s] = v
                self.n += 1

    def _record(self, tok, reads, writes):
        for k in reads:
            r = self.readers.setdefault(k, {})
            if r.get(tok[0], 0) < tok[1]:
                r[tok[0]] = tok[1]
        for k in writes:
            self.lastw[k] = tok
            self.readers[k] = {}

    def op(self, en, fn, reads=(), writes=(), inc=True):
        self._waits(en, reads, writes)
        E = self.eng[en]
        ins = fn(E["e"])
        self.n += 1
        if inc:
            E["cnt"] += 1
            ins.then_inc(self.sems[en], 1)
            tok = (en, E["cnt"])
        else:
            tok = (en, E["cnt"] + 1)
        self._record(tok, reads, writes)

    def dma(self, q, out, in_, reads=(), writes=(), key=None):
        self._waits(q, reads, writes)
        if key not in self.sems:
            self.sems[key] = self.es.enter_context(self.nc.semaphore("d_" + str(len(self.sems))))
            self.dcnt[key] = 0
        self.dcnt[key] += 16
        self.eng[q]["e"].dma_start(out=out, in_=in_).then_inc(self.sems[key], 16)
        self.n += 1
        self._record((key, self.dcnt[key]), reads, writes)

    def barrier(self):
        for en, E in self.eng.items():
            for s in self.sems:
                v = self.eng[s]["cnt"] if s in self.eng else self.dcnt[s]
                if s == en or v == 0:
                    continue
                if E["seen"].get(s, 0) < v:
                    E["e"].wait_ge(self.sems[s], v)
                    E["seen"][s] = v
        self.lastw.clear()
        self.readers.clear()


def build(S, stop_after=99, dbg=False):
    NT = S // T
    NKC = S // 128
    nc = bass.Bass("TRN2", target_bir_lowering=False)

    def dram(name, shape, dt, kind="Internal"):
        if dbg and kind == "Internal" and not name.startswith("pk_"):
            kind = "ExternalOutput"
        return nc.dram_tensor(name, list(shape), dt, kind=kind).ap()

    x_in = dram("x", [S, D], F32, "ExternalInput")
    y_out = dram("y", [S, D], F32, "ExternalOutput")
    Wd = {n: dram(n, s, F32, "ExternalInput") for n, s in WSHAPES.items()}
    cvec_d = dram("cvec", [128, NCV], F32, "ExternalInput")
    cmat_d = dram("cmat", [128, 4, 128], F32, "ExternalInput")
    ropeA_d = dram("ropeA", [2, 128, S], F32, "ExternalInput")
    ropeB_d = dram("ropeB", [2, 64, S], F32, "ExternalInput")
    invc_d = dram("invc", [4, 128, S], F32, "ExternalInput")

    Pk = {}
    for p in FFNS:
        Pk[p + "_w_gate"] = dram("pk_" + p + "_g", [FC, 128, DC * 128], BF16)
        Pk[p + "_w_up"] = dram("pk_" + p + "_u", [FC, 128, DC * 128], BF16)
        Pk[p + "_w_down"] = dram("pk_" + p + "_d", [DC, 128, FC * 128], BF16)
    Pk["l0_w_in"] = dram("pk_win", [19, 128, DC * 128], BF16)
    Pk["l0_b_w_uq"] = dram("pk_uq", [8, 128, 4 * 192], BF16)
    Pk["l0_b_w_ukv"] = dram("pk_ukv", [16, 128, 2 * 128], BF16)
    Pk["l0_w_out"] = dram("pk_wout", [16, 128, DC * 128], BF16)
    Pk["l1_pool_w"] = dram("pk_pool", [16, 128, 4 * 128], BF16)
    QA = dram("s_QA", [8, 128, S], BF16)
    KA = dram("s_KA", [2, 128, S], BF16)
    VA = dram("s_VA", [S, 256], BF16)
    QBn = dram("s_QBn", [8, 128, S], BF16)
    QBr = dram("s_QBr", [8, 64, S], BF16)
    KBn = dram("s_KBn", [8, 128, S], BF16)
    KBr = dram("s_KBr", [64, S], BF16)
    VB = dram("s_VB", [S, 1024], BF16)
    ATT = dram("s_ATT", [16, 128, S], BF16)
    X1 = dram("s_X1", [16, 128, S], F32)
    X4 = dram("s_X4", [16, 128, S], F32)
    HN4 = dram("s_HN4", [16, 128, S], F32)

    with ExitStack() as es:
        sc = Sched(nc, es)
        cvec = es.enter_context(nc.sbuf_tensor("cvec_sb", [128, NCV], F32))
        cmat = es.enter_context(nc.sbuf_tensor("cmat_sb", [128, 4, 128], F32))
        onesb = es.enter_context(nc.sbuf_tensor("onesb", [128, 128], BF16))
        PS = [es.enter_context(nc.psum_tensor(f"ps{i}", [128, 512], F32)) for i in range(8)]
        PK = [f"ps{i}" for i in range(8)]
        sc.dma("sp", cvec[:], cvec_d, writes=["cvec"], key="k_c")
        sc.dma("sp", cmat[:], cmat_d, writes=["cmat"], key="k_c")
        sc.op("dve", lambda e: e.tensor_copy(out=onesb[:], in_=cmat[:, 0, :]), reads=["cmat"], writes=["onesb"])
        pg2 = es.enter_context(nc.sbuf_tensor("pg2", [128, DC], F32))
        sc.op("dve", lambda e: e.tensor_tensor(out=pg2[:], in0=cvec[:, VCOL["l1_pool_scale"]:VCOL["l1_pool_scale"] + DC],
                                               in1=cvec[:, VCOL["l1_mix_post_g"]:VCOL["l1_mix_post_g"] + DC], op=ALU.mult),
              reads=["cvec"], writes=["pg2"])
        ONES = cmat[:, 0, :]
        IDENT = cmat[:, 1, :]
        PERMA = cmat[:, 2, :]
        PERMB = cmat[0:64, 3, 0:64]

        def gv(name, c):
            return cvec[:, VCOL[name] + c: VCOL[name] + c + 1]

        def slab_list(src, dst, K, M, CW, cap, row0=0, dch0=0):
            nk = K // 128
            nch = (M + CW - 1) // CW
            G = max(1, min(nch, cap // (nk * CW)))
            out = []
            c0 = 0
            while c0 < nch:
                g = min(G, nch - c0)
                w = min(M, (c0 + g) * CW) - c0 * CW
                if w < g * CW and g > 1:
                    g -= 1
                    w = g * CW
                out.append(dict(src=src, dst=dst, K=K, CW=CW, nk=nk, g=g, w=w, c0=c0, row0=row0, dch0=dch0))
                c0 += g
            return out

        def slab_load(sl, stg_t, key):
            nk, w = sl["nk"], sl["w"]
            sv = stg_t[:, 0:nk * w].rearrange("p (k m) -> p k m", k=nk)
            srcv = sl["src"][sl["row0"]:sl["row0"] + sl["K"], :].rearrange("(k p) m -> p k m", p=128)[:, :, sl["c0"] * sl["CW"]:sl["c0"] * sl["CW"] + w]
            sc.dma("sp", sv, srcv, writes=[key], key="k_" + key)

        def slab_cast(sl, stg_t, skey, stb_t, bkey, en, pieces=1):
            nk, w, g, CW = sl["nk"], sl["w"], sl["g"], sl["CW"]
            cw = w // g
            sv = stg_t[:, 0:nk * w].rearrange("p (k m) -> p k m", k=nk)
            ov = stb_t[:, 0:g * nk * CW].rearrange("p (g k c) -> p g k c", g=g, k=nk)[:, :, :, 0:cw]
            iv = sv.rearrange("p k (g c) -> p g k c", g=g)
            pieces = min(pieces, nk)
            step = (nk + pieces - 1) // pieces
            for k0 in range(0, nk, step):
                k1 = min(nk, k0 + step)
                o_, i_ = ov[:, :, k0:k1, :], iv[:, :, k0:k1, :]
                if en == "act":
                    sc.op("act", lambda e: e.activation(out=o_, in_=i_, func=AF.Copy), reads=[skey], writes=[bkey])
                else:
                    sc.op(en, lambda e: e.tensor_copy(out=o_, in_=i_), reads=[skey], writes=[bkey])

        def slab_store(sl, stb_t, bkey):
            nk, g, CW = sl["nk"], sl["g"], sl["CW"]
            dv = sl["dst"][sl["dch0"] + sl["c0"]:sl["dch0"] + sl["c0"] + g].rearrange("g p x -> p g x")
            sc.dma("pool", dv, stb_t[:, 0:g * nk * CW].rearrange("p (g x) -> p g x", g=g), reads=[bkey], writes=["pk"], key="k_" + bkey)

        def conv_specs(names, cap):
            L = []
            for nm in names:
                if nm == "l1_pool_w":
                    for g in range(4):
                        L += slab_list(Wd[nm], Pk[nm], 512, 512, 128, cap, row0=g * 512, dch0=g * 4)
                else:
                    K_, M_ = WSHAPES[nm]
                    L += slab_list(Wd[nm], Pk[nm], K_, M_, 192 if nm == "l0_b_w_uq" else 128, cap)
            return L

        EARLY = ["l0_ffn1_w_gate", "l0_ffn1_w_up", "l0_ffn1_w_down", "l0_w_in", "l0_b_w_uq", "l0_b_w_ukv"]
        LATE = [p + sfx for p in FFNS[1:] for sfx in ("_w_gate", "_w_up", "_w_down")] + ["l0_w_out", "l1_pool_w"]

        with ExitStack() as ph:
            stg = [ph.enter_context(nc.sbuf_tensor(f"stg{i}", [128, 8192], F32)) for i in range(2)]
            stb = [ph.enter_context(nc.sbuf_tensor(f"stb{i}", [128, 8192], BF16)) for i in range(2)]
            for n_, sl in enumerate(conv_specs(EARLY, 8192)):
                i = n_ % 2
                slab_load(sl, stg[i], f"stg{i}")
                slab_cast(sl, stg[i], f"stg{i}", stb[i], f"stb{i}", "act" if n_ % 2 else "dve")
                slab_store(sl, stb[i], f"stb{i}")
            sc.barrier()

        if stop_after < 1:
            sc.barrier()
            return nc

        def alloc_ffn(ph, tg):
            B = {}
            B["xres"] = ph.enter_context(nc.sbuf_tensor("xres" + tg, [128, DC, T], F32))
            B["xn"] = ph.enter_context(nc.sbuf_tensor("xn" + tg, [128, DC, T], BF16))
            B["hT"] = ph.enter_context(nc.sbuf_tensor("hT" + tg, [128, FC, T], BF16))
            B["y"] = ph.enter_context(nc.sbuf_tensor("ybuf" + tg, [128, DC, T], F32))
            B["wg"] = [ph.enter_context(nc.sbuf_tensor(f"wg{i}" + tg, [128, DC, 128], BF16)) for i in range(2)]
            B["wu"] = [ph.enter_context(nc.sbuf_tensor(f"wu{i}" + tg, [128, DC, 128], BF16)) for i in range(2)]
            B["wd"] = [ph.enter_context(nc.sbuf_tensor(f"wd{i}" + tg, [128, 6144], BF16)) for i in range(2)]
            B["sm"] = [ph.enter_context(nc.sbuf_tensor(f"sm{i}" + tg, [128, T], F32)) for i in range(16)]
            B["sqi"] = 0
            B["wgi"] = 0
            B["wdi"] = 0
            return B

        XR = [("xres", c) for c in range(DC)]
        XN = [("xn", c) for c in range(DC)]
        YK = [("y", c) for c in range(DC)]
        HK = [("hT", j) for j in range(FC)]

        def rstd_from_sumsq(B, ps_i, n, out_i, half=False, npart=128):
            sm = B["sm"]
            sc.op("dve", lambda e: e.tensor_scalar(out=sm[3][0:npart, :], in0=PS[ps_i][0:npart, :], scalar1=1.0 / n, scalar2=EPS,
                                                   op0=ALU.mult, op1=ALU.add), reads=[PK[ps_i]], writes=["sm3"])
            sc.op("act", lambda e: e.activation(out=sm[4][0:npart, :], in_=sm[3][0:npart, :], func=AF.Sqrt,
                                                scale=(4.0 if half else 1.0)), reads=["sm3"], writes=["sm4"])
            sc.op("dve", lambda e: e.reciprocal(out=sm[out_i][0:npart, :], in_=sm[4][0:npart, :]), reads=["sm4"], writes=[f"sm{out_i}"])

        SQR = [0, 1, 14, 15]

        def sq_issue(B, src_ap, skeys, first, scale=None):
            sm = B["sm"]
            kw = {} if scale is None else {"scale": scale}
            if first:
                sc.op("act", lambda e: e.activation(out=sm[2][:], in_=src_ap, func=AF.Square, **kw), reads=skeys, writes=["sm2"])
                return None
            i = SQR[B["sqi"] % 4]
            B["sqi"] += 1
            sc.op("act", lambda e: e.activation(out=sm[i][:], in_=src_ap, func=AF.Square, **kw), reads=skeys, writes=[f"sm{i}"])
            return i

        def sq_add(B, i):
            sm = B["sm"]
            if i is None:
                return
            sc.op("dve", lambda e: e.tensor_tensor(out=sm[2][:], in0=sm[2][:], in1=sm[i][:], op=ALU.add), reads=[f"sm{i}", "sm2"], writes=["sm2"])

        def sq_acc(B, src_ap, skeys, first, scale=None):
            sq_add(B, sq_issue(B, src_ap, skeys, first, scale))

        def finish_stats(B, n, out_i, half=False, ps_i=7):
            sm = B["sm"]
            sc.op("pe", lambda e: e.matmul(PS[ps_i][:], lhsT=ONES, rhs=sm[2][:], start=True, stop=True), reads=["sm2", "cmat"], writes=[PK[ps_i]])
            rstd_from_sumsq(B, ps_i, n, out_i, half)

        def norm_stats(B, src, keys, nch, n, out_i, half=False, ps_i=7):
            for c in range(nch):
                sq_acc(B, src[:, c, :], [keys[c]], c == 0)
            finish_stats(B, n, out_i, half, ps_i)

        def apply_norm(B, src, skeys, gname, rs_i, dst, dkeys, nch=DC):
            sm = B["sm"]
            for c in range(nch):
                sc.op("dve", lambda e: e.scalar_tensor_tensor(out=dst[:, c, :], in0=src[:, c, :], scalar=gv(gname, c), in1=sm[rs_i][:],
                                                              op0=ALU.mult, op1=ALU.mult), reads=[skeys[c], f"sm{rs_i}", "cvec"], writes=[dkeys[c]])

        LAG = 3

        def ffn_prep_xn(B, p, c):
            xres, xn = B["xres"], B["xn"]
            sc.op("act", lambda e: e.activation(out=xn[:, c, :], in_=xres[:, c, :], func=AF.Identity, scale=gv(p + "_pre_g", c)),
                  reads=[XR[c], "cvec"], writes=[XN[c]])

        def tail(B, rs_i, next_p=None, stats=False):
            y, xres, sm = B["y"], B["xres"], B["sm"]
            want = next_p is not None or stats
            for c in range(DC):
                sc.op("dve", lambda e: e.tensor_tensor(out=y[:, c, :], in0=y[:, c, :], in1=sm[rs_i][:], op=ALU.mult),
                      reads=[YK[c], f"sm{rs_i}"], writes=[YK[c]])
                sc.op("dve", lambda e: e.tensor_tensor(out=xres[:, c, :], in0=xres[:, c, :], in1=y[:, c, :], op=ALU.add),
                      reads=[YK[c], XR[c]], writes=[XR[c]])
                if next_p is not None:
                    ffn_prep_xn(B, next_p, c)
                if want:
                    i = SQR[B["sqi"] % 4]
                    B["sqi"] += 1
                    sc.op("act", lambda e: e.activation(out=sm[i][:], in_=xres[:, c, :], func=AF.Square), reads=[XR[c]], writes=[f"sm{i}"])
                    sc.op("pe", lambda e: e.matmul(PS[7][:], lhsT=ONES, rhs=sm[i][:], start=(c == 0), stop=(c == DC - 1)),
                          reads=[f"sm{i}", "cmat"], writes=[PK[7]])
            if want:
                rstd_from_sumsq(B, 7, D, 5)

        def load_w(B, which, src_chunk):
            shp = list(src_chunk.shape)
            n = int(np.prod(shp[1:]))
            if which == "d":
                i = B["wdi"] % 2
                B["wdi"] += 1
                sc.dma("sp", B["wd"][i][:, 0:n], src_chunk, writes=[f"wd{i}"], key=f"k_wd{i}")
                return B["wd"][i], f"wd{i}"
            i = B["wgi"] % 4
            B["wgi"] += 1
            t = B["wg"][i // 2] if i % 2 == 0 else B["wu"][i // 2]
            k = f"wgu{i}"
            ov = t[:].rearrange("p k c -> p (k c)")[:, 0:n]
            if len(shp) == 3:
                ov = ov.rearrange("p (m x) -> p m x", m=shp[1])
            sc.dma("sp", ov, src_chunk, writes=[k], key="k_" + k)
            return t, k

        def ffn(B, p):
            xres, xn, hT, y, sm = B["xres"], B["xn"], B["hT"], B["y"], B["sm"]
            for j in range(FC):
                wg, kg = load_w(B, "g", Pk[p + "_w_gate"][j])
                wu, ku = load_w(B, "g", Pk[p + "_w_up"][j])
                pg, pu = (j % 2), 2 + (j % 2)
                for kc in range(DC):
                    sc.op("pe", lambda e: e.matmul(PS[pg][:], lhsT=wg[:, kc, :], rhs=xn[:, kc, :], start=(kc == 0), stop=(kc == DC - 1)),
                          reads=[kg, XN[kc]], writes=[PK[pg]], inc=(kc == DC - 1))
                for kc in range(DC):
                    sc.op("pe", lambda e: e.matmul(PS[pu][:], lhsT=wu[:, kc, :], rhs=xn[:, kc, :], start=(kc == 0), stop=(kc == DC - 1)),
                          reads=[ku, XN[kc]], writes=[PK[pu]], inc=(kc == DC - 1))
                a, b_, c_ = 8 + j % 2, 6 + j % 2, 10 + j % 2
                sc.op("dve", lambda e: e.tensor_tensor(out=sm[a][:], in0=PS[pg][:], in1=sm[5][:], op=ALU.mult), reads=[PK[pg], "sm5"], writes=[f"sm{a}"])
                sc.op("act", lambda e: e.activation(out=sm[b_][:], in_=sm[a][:], func=AF.Silu), reads=[f"sm{a}"], writes=[f"sm{b_}"])
                sc.op("pool", lambda e: e.tensor_tensor(out=sm[c_][:], in0=sm[b_][:], in1=sm[5][:], op=ALU.mult), reads=[f"sm{b_}", "sm5"], writes=[f"sm{c_}"])
                sc.op("dve", lambda e: e.tensor_tensor(out=hT[:, j, :], in0=sm[c_][:], in1=PS[pu][:], op=ALU.mult),
                      reads=[f"sm{c_}", PK[pu]], writes=[HK[j]])
            for m in range(DC):
                wd, kd = load_w(B, "d", Pk[p + "_w_down"][m])
                pd = 4 + (m % 2)
                for j in range(FC):
                    sc.op("pe", lambda e: e.matmul(PS[pd][:], lhsT=wd[:, j * 128:(j + 1) * 128], rhs=hT[:, j, :], start=(j == 0), stop=(j == FC - 1)),
                          reads=[kd, HK[j]], writes=[PK[pd]], inc=(j == FC - 1))
                sc.op("act", lambda e: e.activation(out=y[:, m, :], in_=PS[pd][:], func=AF.Identity, scale=gv(p + "_post_g", m)),
                      reads=[PK[pd], "cvec"], writes=[YK[m]])
                sq_acc(B, PS[pd][:], [PK[pd]], m == 0)
            finish_stats(B, D, 12, half=True)

        def load_x_tokmajor(B, t):
            y, xres = B["y"], B["xres"]
            for s4 in range(4):
                stv = y[:, 4 * s4:4 * s4 + 4, :].rearrange("p c t -> p (c t)")
                sc.dma("pool", stv, x_in[t * T + s4 * 128: t * T + (s4 + 1) * 128, :], writes=YK[4 * s4:4 * s4 + 4], key=f"k_xs{s4}")
            slots = {}
            for c in range(DC + LAG):
                if c < DC:
                    pi = c % 2
                    for s4 in range(4):
                        stv = y[:, 4 * s4:4 * s4 + 4, :].rearrange("p c t -> p (c t)")
                        sc.op("pe", lambda e: e.transpose(out=PS[pi][:, s4 * 128:(s4 + 1) * 128], in_=stv[:, c * 128:(c + 1) * 128], identity=IDENT),
                              reads=YK[4 * s4:4 * s4 + 4] + ["cmat"], writes=[PK[pi]], inc=(s4 == 3))
                    sc.op("dve", lambda e: e.tensor_copy(out=xres[:, c, :], in_=PS[pi][:]), reads=[PK[pi]], writes=[XR[c]])
                    ffn_prep_xn(B, "l0_ffn1", c)
                    slots[c] = sq_issue(B, xres[:, c, :], [XR[c]], c == 0)
                if c - LAG >= 0:
                    sq_add(B, slots[c - LAG])
            finish_stats(B, D, 5)

        def store_fm(B, src, keys, dst, t, tag):
            sc.dma("pool", dst[:, :, t * T:(t + 1) * T].rearrange("c p t -> p c t"), src[:], reads=keys, writes=[tag], key="k_st_" + tag)

        def load_fm(B, dstt, keys, src, t, tag):
            sc.dma("pool", dstt[:], src[:, :, t * T:(t + 1) * T].rearrange("c p t -> p c t"), reads=[tag], writes=keys, key="k_ld_" + tag)

        with ExitStack() as ph:
            B = alloc_ffn(ph, "_a")
            sm, y, hT, xn, xres = B["sm"], B["y"], B["hT"], B["xn"], B["xres"]
            Win = Pk["l0_w_in"]
            for t in range(NT):
                tsl = slice(t * T, (t + 1) * T)
                load_x_tokmajor(B, t)
                ffn(B, "l0_ffn1")
                tail(B, 12, stats=True)
                store_fm(B, xres, XR, X1, t, "X1")
                if stop_after < 2:
                    continue
                sc.dma("pool", sm[8][:], ropeA_d[0, :, tsl], writes=["sm8"], key="k_r0")
                sc.dma("pool", sm[9][:], ropeA_d[1, :, tsl], writes=["sm9"], key="k_r1")
                sc.dma("pool", sm[10][0:64, :], ropeB_d[0, :, tsl], writes=["sm10"], key="k_r2")
                sc.dma("pool", sm[11][0:64, :], ropeB_d[1, :, tsl], writes=["sm11"], key="k_r3")
                B["wdi"] = 0
                sc.dma("sp", B["wd"][0][:, 0:8 * 768].rearrange("p (h x) -> p h x", h=8), Pk["l0_b_w_uq"].rearrange("h p x -> p h x"),
                       writes=["wd0"], key="k_wd0")
                sc.dma("sp", B["wd"][1][:, 0:16 * 256].rearrange("p (h x) -> p h x", h=16), Pk["l0_b_w_ukv"].rearrange("h p x -> p h x"),
                       writes=["wd1"], key="k_wd1")
                wuq = B["wd"][0][:, 0:8 * 768].rearrange("p (h k c) -> p h k c", h=8, k=4)
                wukv = B["wd"][1][:, 0:16 * 256].rearrange("p (m k c) -> p m k c", m=16, k=2)
                apply_norm(B, xres, XR, "l0_mix_pre_g", 5, xn, XN)

                def proj(mc, ps_i, M=128):
                    w, kw = load_w(B, "g", Win[mc])
                    for kc in range(DC):
                        sc.op("pe", lambda e: e.matmul(PS[ps_i][0:M, :], lhsT=w[:, kc, 0:M], rhs=xn[:, kc, :], start=(kc == 0), stop=(kc == DC - 1)),
                              reads=[kw, XN[kc]], writes=[PK[ps_i]], inc=(kc == DC - 1))

                def rope(src_ap, skey, np_, perm, ci, si, dst_ap, dkey, ytmp):
                    sc.op("pe", lambda e: e.matmul(PS[6][0:np_, :], lhsT=perm, rhs=src_ap, start=True, stop=True), reads=[skey, "cmat"], writes=[PK[6]])
                    sc.op("pool", lambda e: e.tensor_tensor(out=y[0:np_, ytmp, :], in0=src_ap, in1=sm[ci][0:np_, :], op=ALU.mult),
                          reads=[skey, f"sm{ci}"], writes=[YK[ytmp]])
                    sc.op("dve", lambda e: e.tensor_tensor(out=y[0:np_, ytmp + 1, :], in0=PS[6][0:np_, :], in1=sm[si][0:np_, :], op=ALU.mult),
                          reads=[PK[6], f"sm{si}"], writes=[YK[ytmp + 1]])
                    sc.op("dve", lambda e: e.tensor_tensor(out=dst_ap, in0=y[0:np_, ytmp, :], in1=y[0:np_, ytmp + 1, :], op=ALU.add),
                          reads=[YK[ytmp], YK[ytmp + 1]], writes=[dkey])

                def a_stage0(mc):
                    pi = mc % 3
                    yr = 4 * (mc % 3)
                    proj(mc, pi)
                    sc.op("act", lambda e: e.activation(out=y[:, yr, :], in_=PS[pi][:], func=AF.Copy), reads=[PK[pi]], writes=[YK[yr]])
                    sq = 2 if mc % 2 == 0 else 13
                    sc.op("act", lambda e: e.activation(out=sm[sq][:], in_=PS[pi][:], func=AF.Square), reads=[PK[pi]], writes=[f"sm{sq}"])

                def a_stage1(mc):
                    yr = 4 * (mc % 3)
                    sq = 2 if mc % 2 == 0 else 13
                    sc.op("pe", lambda e: e.matmul(PS[7][:], lhsT=ONES, rhs=sm[sq][:], start=True, stop=True), reads=[f"sm{sq}", "cmat"], writes=[PK[7]])
                    rstd_from_sumsq(B, 7, 128, 5)
                    gname = "l0_a_q_norm_g" if mc < 8 else "l0_a_k_norm_g"
                    sc.op("dve", lambda e: e.scalar_tensor_tensor(out=y[:, yr + 1, :], in0=y[:, yr, :], scalar=gv(gname, 0), in1=sm[5][:],
                                                                  op0=ALU.mult, op1=ALU.mult), reads=[YK[yr], "sm5", "cvec"], writes=[YK[yr + 1]])

                def a_stage2(mc):
                    yr = 4 * (mc % 3)
                    ob = hT[:, mc, :]
                    rope(y[:, yr + 1, :], YK[yr + 1], 128, PERMA, 8, 9, ob, HK[mc], yr + 2)
                    dst = QA[mc, :, tsl] if mc < 8 else KA[mc - 8, :, tsl]
                    sc.dma("pool", dst, ob, reads=[HK[mc]], writes=["QKA"], key=f"k_o{mc % 4}")

                for i in range(12):
                    if i < 10:
                        a_stage0(i)
                    if 0 <= i - 1 < 10:
                        a_stage1(i - 1)
                    if 0 <= i - 2 < 10:
                        a_stage2(i - 2)
                def lowrank(mcs, gname, n, ybase, hbase):
                    nch = len(mcs)
                    for i, mc in enumerate(mcs):
                        pi = i % 2
                        proj(mc, pi)
                        sc.op("act", lambda e: e.activation(out=y[:, ybase + i, :], in_=PS[pi][:], func=AF.Copy), reads=[PK[pi]], writes=[YK[ybase + i]])
                    norm_stats(B, y[:, ybase:ybase + nch, :], YK[ybase:ybase + nch], nch, n, 5)
                    for i in range(nch):
                        sc.op("dve", lambda e: e.scalar_tensor_tensor(out=hT[:, hbase + i, :], in0=y[:, ybase + i, :], scalar=gv(gname, i), in1=sm[5][:],
                                                                      op0=ALU.mult, op1=ALU.mult), reads=[YK[ybase + i], "sm5", "cvec"], writes=[HK[hbase + i]])

                lowrank([12, 13, 14, 15], "l0_b_cq_norm_g", 512, 8, 12)
                for vi in range(2):
                    w, kw = load_w(B, "g", Win[10 + vi])
                    for s4 in range(4):
                        for kc in range(DC):
                            sc.op("pe", lambda e: e.matmul(PS[vi][:, s4 * 128:(s4 + 1) * 128], lhsT=xn[:, kc, s4 * 128:(s4 + 1) * 128], rhs=w[:, kc, :],
                                                           start=(kc == 0), stop=(kc == DC - 1)), reads=[kw, XN[kc]], writes=[PK[vi]],
                                  inc=(kc == DC - 1 and s4 == 3))
                    ob = hT[:, 10 + vi, :]
                    sc.op("act", lambda e: e.activation(out=ob, in_=PS[vi][:], func=AF.Copy), reads=[PK[vi]], writes=[HK[10 + vi]])
                    sc.dma("pool", VA[tsl, vi * 128:(vi + 1) * 128].rearrange("(s p) d -> p s d", p=128),
                           ob.rearrange("p (s d) -> p s d", s=4), reads=[HK[10 + vi]], writes=["VA"], key=f"k_o{vi}")

                lowrank([16, 17], "l0_b_ckv_norm_g", 256, 12, 16)
                proj(18, 4, M=64)
                sc.op("act", lambda e: e.activation(out=y[0:64, 0, :], in_=PS[4][0:64, :], func=AF.Copy), reads=[PK[4]], writes=[YK[0]])
                ob2 = hT[0:64, 36, :]
                rope(y[0:64, 0, :], YK[0], 64, PERMB, 10, 11, ob2, HK[36], 1)
                sc.dma("pool", KBr[:, tsl], ob2, reads=[HK[36]], writes=["KBr"], key="k_kr")
                for h in range(8):
                    pi = h % 2
                    for kc in range(4):
                        sc.op("pe", lambda e: e.matmul(PS[pi][:], lhsT=wuq[:, h, kc, 0:128], rhs=hT[:, 12 + kc, :], start=(kc == 0), stop=(kc == 3)),
                              reads=["wd0", HK[12 + kc]], writes=[PK[pi]], inc=(kc == 3))
                    ob = hT[:, 20 + (h % 4), :]
                    sc.op("act", lambda e: e.activation(out=ob, in_=PS[pi][:], func=AF.Copy), reads=[PK[pi]], writes=[HK[20 + h % 4]])
                    sc.dma("pool", QBn[h, :, tsl], ob, reads=[HK[20 + h % 4]], writes=["QBn"], key=f"k_o{h % 4}")
                    pj = 2 + h % 2
                    for kc in range(4):
                        sc.op("pe", lambda e: e.matmul(PS[pj][0:64, :], lhsT=wuq[:, h, kc, 128:192], rhs=hT[:, 12 + kc, :], start=(kc == 0), stop=(kc == 3)),
                              reads=["wd0", HK[12 + kc]], writes=[PK[pj]], inc=(kc == 3))
                    yr = 4 * (h % 2)
                    sc.op("act", lambda e: e.activation(out=y[0:64, yr, :], in_=PS[pj][0:64, :], func=AF.Copy), reads=[PK[pj]], writes=[YK[yr]])
                    ob2 = hT[0:64, 24 + (h % 4), :]
                    rope(y[0:64, yr, :], YK[yr], 64, PERMB, 10, 11, ob2, HK[24 + h % 4], yr + 1)
                    sc.dma("pool", QBr[h, :, tsl], ob2, reads=[HK[24 + h % 4]], writes=["QBr"], key=f"k_p{h % 4}")
                for h in range(8):
                    pi = h % 2
                    for kc in range(2):
                        sc.op("pe", lambda e: e.matmul(PS[pi][:], lhsT=wukv[:, 2 * h, kc, :], rhs=hT[:, 16 + kc, :], start=(kc == 0), stop=(kc == 1)),
                              reads=["wd1", HK[16 + kc]], writes=[PK[pi]], inc=(kc == 1))
                    ob = hT[:, 28 + (h % 4), :]
                    sc.op("act", lambda e: e.activation(out=ob, in_=PS[pi][:], func=AF.Copy), reads=[PK[pi]], writes=[HK[28 + h % 4]])
                    sc.dma("pool", KBn[h, :, tsl], ob, reads=[HK[28 + h % 4]], writes=["KBn"], key=f"k_q{h % 4}")
                for s4 in range(4):
                    for hh in range(2):
                        pi = 2 + hh
                        for kc in range(2):
                            sc.op("pe", lambda e: e.matmul(PS[pi][:].rearrange("p (h c) -> p h c", h=4), lhsT=hT[:, 16 + kc, s4 * 128:(s4 + 1) * 128],
                                                           rhs=wukv[:, 8 * hh + 1:8 * hh + 8:2, kc, :], start=(kc == 0), stop=(kc == 1)),
                                  reads=["wd1", HK[16 + kc]], writes=[PK[pi]], inc=(kc == 1))
                        ob = hT[:, 32 + 2 * (s4 % 2) + hh, :]
                        hk = HK[32 + 2 * (s4 % 2) + hh]
                        sc.op("act", lambda e: e.activation(out=ob, in_=PS[pi][:], func=AF.Copy), reads=[PK[pi]], writes=[hk])
                        sc.dma("pool", VB[t * T + s4 * 128:t * T + (s4 + 1) * 128, hh * 512:(hh + 1) * 512], ob, reads=[hk], writes=["VB"],
                               key=f"k_v{(2 * s4 + hh) % 4}")
            sc.barrier()

        if stop_after < 3:
            sc.barrier()
            return nc

        with ExitStack() as ph:
            KT = [ph.enter_context(nc.sbuf_tensor(f"KT{i}", [128, S], BF16)) for i in range(2)]
            VT = [ph.enter_context(nc.sbuf_tensor(f"VT{i}", [128, NKC, 128], BF16)) for i in range(2)]
            KR = ph.enter_context(nc.sbuf_tensor("KR", [128, S], BF16))
            QT = [ph.enter_context(nc.sbuf_tensor(f"QT{i}", [128, T], BF16)) for i in range(3)]
            QR = [ph.enter_context(nc.sbuf_tensor(f"QR{i}", [128, T], BF16)) for i in range(3)]
            PT = [ph.enter_context(nc.sbuf_tensor(f"PT{i}", [128, T], BF16)) for i in range(8)]
            RI = [ph.enter_context(nc.sbuf_tensor(f"RI{i}", [128, T], F32)) for i in range(2)]
            OB = [ph.enter_context(nc.sbuf_tensor(f"OB{i}", [128, T], BF16)) for i in range(2)]
            ACC = [ph.enter_context(nc.sbuf_tensor(f"ACC{i}", [128, T], F32)) for i in range(8)]
            sc.op("pool", lambda e: e.memset(KR[64:128, :], 0.0), writes=["KR"])
            for i in range(3):
                sc.op("pool", lambda e: e.memset(QR[i][64:128, :], 0.0), writes=[f"QR{i}"])
            sc.dma("pool", KR[0:64, :], KBr, reads=["KBr"], writes=["KR"], key="k_KR")
            NQT = S // T
            items = [(hd, qt) for hd in range(16) for qt in range(NQT)]
            LCAP = 5632
            stgL = [ph.enter_context(nc.sbuf_tensor(f"stgL{i}", [128, LCAP], F32)) for i in range(2)]
            stbL = [ph.enter_context(nc.sbuf_tensor(f"stbL{i}", [128, LCAP], BF16)) for i in range(2)]
            late = conv_specs(LATE, LCAP)
            per_item = (len(late) + len(items) - 1) // len(items)
            lpos = [0, 0]

            def late_step(k):
                for _ in range(k):
                    if lpos[1] < lpos[0]:
                        j = lpos[1]
                        i = j % 2
                        slab_cast(late[j], stgL[i], f"stgL{i}", stbL[i], f"stbL{i}", "pool", pieces=4)
                        slab_store(late[j], stbL[i], f"stbL{i}")
                        lpos[1] += 1
                    if lpos[0] < len(late) and lpos[0] - lpos[1] < 2:
                        j = lpos[0]
                        i = j % 2
                        slab_load(late[j], stgL[i], f"stgL{i}")
                        lpos[0] += 1

            def head_loads(hd):
                h, b = hd % 8, hd % 2
                if hd < 8:
                    ksrc = KA[h // 4]
                    vsrc = VA[:, (h // 4) * 128:(h // 4 + 1) * 128].rearrange("(c p) d -> p c d", p=128)
                else:
                    ksrc = KBn[h]
                    vsrc = VB[:, h * 128:(h + 1) * 128].rearrange("(c p) d -> p c d", p=128)
                sc.dma("sp", KT[b][:], ksrc, writes=[f"KT{b}"], key=f"k_KT{b}")
                for v4 in range(0, NKC, 16):
                    v5 = min(NKC, v4 + 16)
                    sc.dma("sp", VT[b][:, v4:v5, :], vsrc[:, v4:v5, :], writes=[f"VT{b}"], key=f"k_VT{b}")

            def q_load(n):
                hd, qt = items[n]
                h, q3 = hd % 8, n % 3
                qsl = slice(qt * T, (qt + 1) * T)
                sc.dma("pool", QT[q3][:], (QBn if hd >= 8 else QA)[h, :, qsl], writes=[f"QT{q3}"], key=f"k_QT{q3}")
                if hd >= 8:
                    sc.dma("pool", QR[q3][0:64, :], QBr[h, :, qsl], writes=[f"QR{q3}"], key=f"k_QR{q3}")

            head_loads(0)
            q_load(0)
            pending = [None]
            for n, (hd, qt) in enumerate(items):
                isB = hd >= 8
                h, b = hd % 8, hd % 2
                scale = (192.0 if isB else 128.0) ** -0.5
                q3, o2 = n % 3, n % 2
                qsl = slice(qt * T, (qt + 1) * T)
                if qt == 0 and hd + 1 < 16:
                    head_loads(hd + 1)
                if n + 1 < len(items):
                    q_load(n + 1)
                late_step(per_item)
                pO, pL = 4 + o2, 6 + o2
                NSB, LA = 4, 3
                st = {"d": 0, "l": 0}

                def s_mm(kc):
                    pi = kc % NSB
                    pt = kc % 8
                    ksl = slice(kc * 128, (kc + 1) * 128)
                    if isB:
                        sc.op("pe", lambda e: e.matmul(PS[pi][:], lhsT=KT[b][:, ksl], rhs=QT[q3][:], start=True, stop=False),
                              reads=[f"KT{b}", f"QT{q3}"], writes=[PK[pi]], inc=False)
                        sc.op("pe", lambda e: e.matmul(PS[pi][:], lhsT=KR[:, ksl], rhs=QR[q3][:], start=False, stop=True),
                              reads=["KR", f"QR{q3}"], writes=[PK[pi]])
                    else:
                        sc.op("pe", lambda e: e.matmul(PS[pi][:], lhsT=KT[b][:, ksl], rhs=QT[q3][:], start=True, stop=True),
                              reads=[f"KT{b}", f"QT{q3}"], writes=[PK[pi]])
                    sc.op("act", lambda e: e.activation(out=PT[pt][:], in_=PS[pi][:], func=AF.Exp, scale=scale), reads=[PK[pi]], writes=[f"PT{pt}"])

                def pv_mm(kc):
                    pt = kc % 8
                    sc.op("pe", lambda e: e.matmul(PS[pO][:], lhsT=VT[b][:, kc, :], rhs=PT[pt][:], start=(kc == 0), stop=(kc == NKC - 1)),
                          reads=[f"VT{b}", f"PT{pt}"], writes=[PK[pO]], inc=True)
                    if (not isB) and kc % 3 == 2:
                        sc.op("pe", lambda e: e.matmul(PS[pL][:], lhsT=onesb[:], rhs=PT[pt][:], start=(st["l"] == 0), stop=False),
                              reads=["onesb", f"PT{pt}"], writes=[PK[pL]], inc=True)
                        st["l"] += 1
                        return
                    a = 4 * o2 + st["d"] % 4
                    if st["d"] < 4:
                        sc.op("dve", lambda e: e.tensor_copy(out=ACC[a][:], in_=PT[pt][:]), reads=[f"PT{pt}"], writes=[f"ACC{a}"])
                    else:
                        sc.op("dve", lambda e: e.tensor_tensor(out=ACC[a][:], in0=ACC[a][:], in1=PT[pt][:], op=ALU.add),
                              reads=[f"PT{pt}", f"ACC{a}"], writes=[f"ACC{a}"])
                    st["d"] += 1

                def make_epi(hd=hd, qsl=qsl, o2=o2, pO=pO, pL=pL, st=st):
                    def epi():
                        na = min(4, st["d"])
                        for a4 in range(na):
                            a = 4 * o2 + a4
                            sc.op("pe", lambda e: e.matmul(PS[pL][:], lhsT=ONES, rhs=ACC[a][:], start=(a4 == 0 and st["l"] == 0), stop=(a4 == na - 1)),
                                  reads=["cmat", f"ACC{a}"], writes=[PK[pL]], inc=(a4 == na - 1))
                        sc.op("dve", lambda e: e.reciprocal(out=RI[o2][:], in_=PS[pL][:]), reads=[PK[pL]], writes=[f"RI{o2}"])
                        sc.op("dve", lambda e: e.tensor_tensor(out=OB[o2][:], in0=PS[pO][:], in1=RI[o2][:], op=ALU.mult),
                              reads=[PK[pO], f"RI{o2}"], writes=[f"OB{o2}"])
                        sc.dma("pool", ATT[hd, :, qsl], OB[o2][:], reads=[f"OB{o2}"], writes=["ATT"], key=f"k_OB{o2}")
                    return epi

                for kc in range(min(LA, NKC)):
                    s_mm(kc)
                for kc in range(NKC):
                    if kc + LA < NKC:
                        s_mm(kc + LA)
                    pv_mm(kc)
                    if kc == min(5, NKC - 2) and pending[0] is not None:
                        pending[0]()
                        pending[0] = None
                if pending[0] is not None:
                    pending[0]()
                pending[0] = make_epi()
            pending[0]()
            while lpos[1] < len(late):
                late_step(1)
            sc.barrier()

        if stop_after < 4:
            sc.barrier()
            return nc

        with ExitStack() as ph:
            B = alloc_ffn(ph, "_b")
            sm, y, hT, xn, xres = B["sm"], B["y"], B["hT"], B["xn"], B["xres"]
            for t in range(NT):
                load_fm(B, xres, XR, X1, t, "X1")
                load_fm(B, xn, XN, ATT, t, "ATT")
                for m in range(DC):
                    w, kw = load_w(B, "g", Pk["l0_w_out"][m])
                    pi = m % 2
                    for kc in range(DC):
                        sc.op("pe", lambda e: e.matmul(PS[pi][:], lhsT=w[:, kc, :], rhs=xn[:, kc, :], start=(kc == 0), stop=(kc == DC - 1)),
                              reads=[kw, XN[kc]], writes=[PK[pi]], inc=(kc == DC - 1))
                    sc.op("act", lambda e: e.activation(out=y[:, m, :], in_=PS[pi][:], func=AF.Identity, scale=gv("l0_mix_post_g", m)),
                          reads=[PK[pi], "cvec"], writes=[YK[m]])
                    sq_acc(B, PS[pi][:], [PK[pi]], m == 0)
                finish_stats(B, D, 12)
                tail(B, 12, next_p="l0_ffn2")
                ffn(B, "l0_ffn2")
                tail(B, 12, next_p="l1_ffn1")
                ffn(B, "l1_ffn1")
                tail(B, 12, stats=True)
                store_fm(B, xres, XR, X4, t, "X4")
                for c in range(DC):
                    sc.op("dve", lambda e: e.scalar_tensor_tensor(out=y[:, c, :], in0=xres[:, c, :], scalar=gv("l1_mix_pre_g", c), in1=sm[5][:],
                                                                  op0=ALU.mult, op1=ALU.mult), reads=[XR[c], "sm5", "cvec"], writes=[YK[c]])
                store_fm(B, y, YK, HN4, t, "HN4")
            sc.barrier()

            if stop_after < 5:
                sc.barrier()
                return nc

            HW = T + 16
            hf = hT[:].rearrange("p c t -> p (c t)").bitcast(F32)
            REG = [hf[:, r * 2560:r * 2560 + 4 * HW].rearrange("p (c t) -> p c t", c=4) for r in range(3)]
            RK = [HK[10 * r:10 * r + 9] for r in range(3)]
            TMP = hf[:, 7680:7680 + 4 * T].rearrange("p (c t) -> p c t", c=4)
            TK4 = [HK[30 + 2 * c4:32 + 2 * c4] for c4 in range(4)]

            def pooling(t):
                for w4 in range(4):
                    sc.dma("pool", sm[8 + w4][:], invc_d[w4, :, t * T:(t + 1) * T], writes=[f"sm{8 + w4}"], key=f"k_r{w4}")
                for g in range(4):
                    wwin = 2 << g
                    hw = wwin // 2
                    R0, R1, R2 = REG
                    lo = t * T - 8
                    hi = (t + 1) * T + 8
                    clo, chi = max(lo, 0), min(hi, S)
                    if clo > lo:
                        sc.op("pool", lambda e: e.memset(R0[:, :, 0:clo - lo], 0.0), writes=RK[0])
                    if chi < hi:
                        sc.op("pool", lambda e: e.memset(R0[:, :, HW - (hi - chi):HW], 0.0), writes=RK[0])
                    sc.dma("pool", R0[:, :, clo - lo:chi - lo], HN4[4 * g:4 * g + 4, :, clo:chi].rearrange("c p t -> p c t"),
                           reads=["HN4"], writes=RK[0], key="k_halo")
                    cur, ck = R0, RK[0]
                    n = HW
                    step = 1
                    lvl = 0
                    while step < wwin:
                        dstR, dk = (R1, RK[1]) if lvl % 2 == 0 else (R2, RK[2])
                        n2 = n - step
                        sc.op("dve", lambda e: e.tensor_tensor(out=dstR[:, :, 0:n2], in0=cur[:, :, 0:n2], in1=cur[:, :, step:step + n2], op=ALU.add),
                              reads=ck, writes=dk)
                        cur, ck, n = dstR, dk, n2
                        step *= 2
                        lvl += 1
                    for c4 in range(4):
                        c = 4 * g + c4
                        sc.op("dve", lambda e: e.tensor_tensor(out=TMP[:, c4, :], in0=cur[:, c4, 8 - hw:8 - hw + T], in1=sm[8 + g][:], op=ALU.mult),
                              reads=ck + [f"sm{8 + g}"], writes=TK4[c4])
                        sc.op("pool", lambda e: e.tensor_tensor(out=xn[:, c, :], in0=TMP[:, c4, :], in1=R0[:, c4, 8:8 + T], op=ALU.subtract),
                              reads=TK4[c4] + RK[0], writes=[XN[c]])

            pooling(0)
            for t in range(NT):
                for g in range(4):
                    w, kw = load_w(B, "g", Pk["l1_pool_w"][4 * g:4 * g + 4].rearrange("m p x -> p m x"))
                    wv = w[:].rearrange("p k c -> p (k c)")[:, 0:2048].rearrange("p (m k c) -> p m k c", m=4, k=4)
                    for m4 in range(4):
                        m = 4 * g + m4
                        pi = m % 2
                        for kc in range(4):
                            sc.op("pe", lambda e: e.matmul(PS[pi][:], lhsT=wv[:, m4, kc, :], rhs=xn[:, 4 * g + kc, :], start=(kc == 0), stop=(kc == 3)),
                                  reads=[kw, XN[4 * g + kc]], writes=[PK[pi]], inc=(kc == 3))
                        sc.op("act", lambda e: e.activation(out=y[:, m, :], in_=PS[pi][:], func=AF.Identity, scale=pg2[:, m:m + 1]),
                              reads=[PK[pi], "pg2"], writes=[YK[m]])
                        sq_acc(B, PS[pi][:], [PK[pi]], m == 0, scale=gv("l1_pool_scale", m))
                load_fm(B, xres, XR, X4, t, "X4")
                finish_stats(B, D, 12)
                tail(B, 12, next_p="l1_ffn2")
                ffn(B, "l1_ffn2")
                tail(B, 12)
                if t + 1 < NT:
                    pooling(t + 1)
                for s4 in range(4):
                    stv = y[:, 4 * s4:4 * s4 + 4, :].rearrange("p c t -> p (c t)")
                    for c4 in range(4):
                        pi = c4 % 2
                        for cc in range(4):
                            c = 4 * c4 + cc
                            sc.op("pe", lambda e: e.transpose(out=PS[pi][:, cc * 128:(cc + 1) * 128], in_=xres[:, c, s4 * 128:(s4 + 1) * 128], identity=IDENT),
                                  reads=[XR[c], "cmat"], writes=[PK[pi]], inc=(cc == 3))
                        sc.op("act", lambda e: e.activation(out=stv[:, c4 * 512:(c4 + 1) * 512], in_=PS[pi][:], func=AF.Copy), reads=[PK[pi]], writes=[YK[4 * s4 + c4]])
                    sc.dma("pool", y_out[t * T + s4 * 128:t * T + (s4 + 1) * 128, :], stv, reads=YK[4 * s4:4 * s4 + 4], writes=["yout"], key=f"k_ys{s4}")
            sc.barrier()
        sc.barrier()
    return nc


def make_consts(S):
    cmat = np.zeros((128, 4, 128), np.float32)
    cmat[:, 0, :] = 1.0
    cmat[:, 1, :] = np.eye(128, dtype=np.float32)
    for i in range(64):
        cmat[2 * i + 1, 2, 2 * i] = -1.0
        cmat[2 * i, 2, 2 * i + 1] = 1.0
    for i in range(32):
        cmat[2 * i + 1, 3, 2 * i] = -1.0
        cmat[2 * i, 3, 2 * i + 1] = 1.0
    tt = np.arange(S)
    row = (tt // GRID_W).astype(np.float32)
    col = (tt % GRID_W).astype(np.float32)

    def tab(rot_dim):
        half = rot_dim // 2
        freqs = (np.float32(10000.0) ** (-np.arange(0, half, 2, dtype=np.float32) / np.float32(half))).astype(np.float32)
        ang = np.concatenate([row[:, None] * freqs, col[:, None] * freqs], axis=-1).astype(np.float32)
        c = np.repeat(np.cos(ang).T, 2, axis=0)
        s = np.repeat(np.sin(ang).T, 2, axis=0)
        return np.ascontiguousarray(np.stack([c, s]).astype(np.float32))

    ropeA = tab(128)
    ropeB = tab(64)
    invc = np.zeros((4, 128, S), np.float32)
    for g, w in enumerate((2, 4, 8, 16)):
        lo = np.clip(tt - w // 2, 0, S)
        hi = np.clip(tt + w // 2, 0, S)
        invc[g, :, :] = (1.0 / (hi - lo).astype(np.float32))[None, :]
    return cmat, ropeA, ropeB, invc


def make_inputs(weights, S):
    cvec = np.zeros((128, NCV), np.float32)
    for n, l in VECS:
        cvec[:, VCOL[n]:VCOL[n] + l // 128] = np.asarray(weights[n], np.float32).reshape(l // 128, 128).T
    cmat, ropeA, ropeB, invc = make_consts(S)
    base = {"cvec": cvec, "cmat": cmat, "ropeA": ropeA, "ropeB": ropeB, "invc": invc}
    for n, s in WSHAPES.items():
        base[n] = np.ascontiguousarray(np.asarray(weights[n], np.float32).reshape(s))
    return base


_NC_CACHE = {}


def run_seqs(seqs, weights, S, n_cores=8, **bk):
    key = (S, tuple(sorted(bk.items())))
    if key not in _NC_CACHE:
        _NC_CACHE[key] = build(S, **bk)
    nc = _NC_CACHE[key]
    base = make_inputs(weights, S)
    in_maps = []
    for i in range(n_cores):
        m = dict(base)
        m["x"] = np.ascontiguousarray(seqs[i % len(seqs)], dtype=np.float32)
        in_maps.append(m)
    res = run_bass_kernel_spmd(nc, in_maps, core_ids=list(range(n_cores)))
    return res


def kernel(**inputs):
    xp = np.asarray(inputs["x_prompt"], np.float32)
    xs = np.asarray(inputs["x_sample"], np.float32)
    S = xp.shape[1]
    seqs = [xp[0], xp[1], xs[0], xs[1], xs[2], xs[3]]
    res = run_seqs(seqs, inputs, S)
    outs = [res.results[i]["y"] for i in range(6)]
    return (np.stack(outs[0:2]).astype(np.float32), np.stack(outs[2:6]).astype(np.float32))
```

```python
import numpy as np
from contextlib import ExitStack
import concourse.bass as bass
import concourse.mybir as mybir
from concourse.bass_utils import run_bass_kernel_spmd

F32 = mybir.dt.float32
BF16 = mybir.dt.bfloat16
AF = mybir.ActivationFunctionType
ALU = mybir.AluOpType

T = 512
D = 2048
DC = 16
DFF = 5632
FC = 44
EPS = 1e-6
GRID_W = 64
IN_COLS = 2368
FFNS = ["l0_ffn1", "l0_ffn2", "l1_ffn1", "l1_ffn2"]

WSHAPES = {}
for _p in FFNS:
    WSHAPES[_p + "_w_gate"] = (D, DFF)
    WSHAPES[_p + "_w_up"] = (D, DFF)
    WSHAPES[_p + "_w_down"] = (DFF, D)
WSHAPES["l0_w_in"] = (D, IN_COLS)
WSHAPES["l0_b_w_uq"] = (512, 1536)
WSHAPES["l0_b_w_ukv"] = (256, 2048)
WSHAPES["l0_w_out"] = (D, D)
WSHAPES["l1_pool_w"] = (4 * 512, 512)

VECS = []
for _p in FFNS:
    VECS += [(_p + "_pre_g", 2048), (_p + "_post_g", 2048)]
VECS += [("l0_mix_pre_g", 2048), ("l0_mix_post_g", 2048), ("l1_mix_pre_g", 2048),
         ("l1_mix_post_g", 2048), ("l1_pool_scale", 2048), ("l0_a_q_norm_g", 128),
         ("l0_a_k_norm_g", 128), ("l0_b_cq_norm_g", 512), ("l0_b_ckv_norm_g", 256)]
VCOL = {}
_c = 0
for _n, _l in VECS:
    VCOL[_n] = _c
    _c += _l // 128
NCV = _c


class Sched:
    def __init__(self, nc, es):
        self.nc, self.es = nc, es
        self.sems = {}
        self.eng = {}
        for name, e in [("pe", nc.tensor), ("act", nc.scalar), ("dve", nc.vector),
                        ("pool", nc.gpsimd), ("sp", nc.sync)]:
            self.sems[name] = es.enter_context(nc.semaphore("sem_" + name))
            self.eng[name] = dict(e=e, cnt=0, seen={})
        self.lastw = {}
        self.readers = {}
        self.dcnt = {}
        self.n = 0

    def _waits(self, en, reads, writes):
        E = self.eng[en]
        need = {}
        for k in reads:
            t = self.lastw.get(k)
            if t is not None and need.get(t[0], 0) < t[1]:
                need[t[0]] = t[1]
        for k in writes:
            t = self.lastw.get(k)
            if t is not None and need.get(t[0], 0) < t[1]:
                need[t[0]] = t[1]
            r = self.readers.get(k)
            if r:
                for s, v in r.items():
                    if need.get(s, 0) < v:
                        need[s] = v
        for s, v in need.items():
            if s == "pe" and en == "pe":
                continue
            if E["seen"].get(s, 0) < v:
                E["e"].wait_ge(self.sems[s], v)
                E["seen"][s] = v
                self.n += 1

    def _record(self, tok, reads, writes):
        for k in reads:
            r = self.readers.setdefault(k, {})
            if r.get(tok[0], 0) < tok[1]:
                r[tok[0]] = tok[1]
        for k in writes:
            self.lastw[k] = tok
            self.readers[k] = {}

    def op(self, en, fn, reads=(), writes=(), inc=True):
        self._waits(en, reads, writes)
        E = self.eng[en]
        ins = fn(E["e"])
        self.n += 1
        if inc:
            E["cnt"] += 1
            ins.then_inc(self.sems[en], 1)
            tok = (en, E["cnt"])
        else:
            tok = (en, E["cnt"] + 1)
        self._record(tok, reads, writes)

    def dma(self, q, out, in_, reads=(), writes=(), key=None):
        self._waits(q, reads, writes)
        if key not in self.sems:
            self.sems[key] = self.es.enter_context(self.nc.semaphore("d_" + str(len(self.sems))))
            self.dcnt[key] = 0
        self.dcnt[key] += 16
        self.eng[q]["e"].dma_start(out=out, in_=in_).then_inc(self.sems[key], 16)
        self.n += 1
        self._record((key, self.dcnt[key]), reads, writes)

    def barrier(self):
        for en, E in self.eng.items():
            for s in self.sems:
                v = self.eng[s]["cnt"] if s in self.eng else self.dcnt[s]
                if s == en or v == 0:
                    continue
                if E["seen"].get(s, 0) < v:
                    E["e"].wait_ge(self.sems[s], v)
                    E["seen"][s] = v
        self.lastw.clear()
        self.readers.clear()


def build(S, stop_after=99, dbg=False):
    NT = S // T
    NKC = S // 128
    nc = bass.Bass("TRN2", target_bir_lowering=False)

    def dram(name, shape, dt, kind="Internal"):
        if dbg and kind == "Internal" and not name.startswith("pk_"):
            kind = "ExternalOutput"
        return nc.dram_tensor(name, list(shape), dt, kind=kind).ap()

    x_in = dram("x", [S, D], F32, "ExternalInput")
    y_out = dram("y", [S, D], F32, "ExternalOutput")
    Wd = {n: dram(n, s, F32, "ExternalInput") for n, s in WSHAPES.items()}
    cvec_d = dram("cvec", [128, NCV], F32, "ExternalInput")
    cmat_d = dram("cmat", [128, 4, 128], F32, "ExternalInput")
    ropeA_d = dram("ropeA", [2, 128, S], F32, "ExternalInput")
    ropeB_d = dram("ropeB", [2, 64, S], F32, "ExternalInput")
    invc_d = dram("invc", [4, 128, S], F32, "ExternalInput")

    Pk = {}
    for p in FFNS:
        Pk[p + "_w_gate"] = dram("pk_" + p + "_g", [FC, 128, DC * 128], BF16)
        Pk[p + "_w_up"] = dram("pk_" + p + "_u", [FC, 128, DC * 128], BF16)
        Pk[p + "_w_down"] = dram("pk_" + p + "_d", [DC, 128, FC * 128], BF16)
    Pk["l0_w_in"] = dram("pk_win", [19, 128, DC * 128], BF16)
    Pk["l0_b_w_uq"] = dram("pk_uq", [8, 128, 4 * 192], BF16)
    Pk["l0_b_w_ukv"] = dram("pk_ukv", [16, 128, 2 * 128], BF16)
    Pk["l0_w_out"] = dram("pk_wout", [16, 128, DC * 128], BF16)
    Pk["l1_pool_w"] = dram("pk_pool", [16, 128, 4 * 128], BF16)
    QA = dram("s_QA", [8, 128, S], BF16)
    KA = dram("s_KA", [2, 128, S], BF16)
    VA = dram("s_VA", [S, 256], BF16)
    QBn = dram("s_QBn", [8, 128, S], BF16)
    QBr = dram("s_QBr", [8, 64, S], BF16)
    KBn = dram("s_KBn", [8, 128, S], BF16)
    KBr = dram("s_KBr", [64, S], BF16)
    VB = dram("s_VB", [S, 1024], BF16)
    ATT = dram("s_ATT", [16, 128, S], BF16)
    X1 = dram("s_X1", [16, 128, S], F32)
    X4 = dram("s_X4", [16, 128, S], F32)
    HN4 = dram("s_HN4", [16, 128, S], F32)

    with ExitStack() as es:
        sc = Sched(nc, es)
        cvec = es.enter_context(nc.sbuf_tensor("cvec_sb", [128, NCV], F32))
        cmat = es.enter_context(nc.sbuf_tensor("cmat_sb", [128, 4, 128], F32))
        onesb = es.enter_context(nc.sbuf_tensor("onesb", [128, 128], BF16))
        PS = [es.enter_context(nc.psum_tensor(f"ps{i}", [128, 512], F32)) for i in range(8)]
        PK = [f"ps{i}" for i in range(8)]
        sc.dma("sp", cvec[:], cvec_d, writes=["cvec"], key="k_c")
        sc.dma("sp", cmat[:], cmat_d, writes=["cmat"], key="k_c")
        sc.op("dve", lambda e: e.tensor_copy(out=onesb[:], in_=cmat[:, 0, :]), reads=["cmat"], writes=["onesb"])
        pg2 = es.enter_context(nc.sbuf_tensor("pg2", [128, DC], F32))
        sc.op("dve", lambda e: e.tensor_tensor(out=pg2[:], in0=cvec[:, VCOL["l1_pool_scale"]:VCOL["l1_pool_scale"] + DC],
                                               in1=cvec[:, VCOL["l1_mix_post_g"]:VCOL["l1_mix_post_g"] + DC], op=ALU.mult),
              reads=["cvec"], writes=["pg2"])
        ONES = cmat[:, 0, :]
        IDENT = cmat[:, 1, :]
        PERMA = cmat[:, 2, :]
        PERMB = cmat[0:64, 3, 0:64]

        def gv(name, c):
            return cvec[:, VCOL[name] + c: VCOL[name] + c + 1]

        def slab_list(src, dst, K, M, CW, cap, row0=0, dch0=0):
            nk = K // 128
            nch = (M + CW - 1) // CW
            G = max(1, min(nch, cap // (nk * CW)))
            out = []
            c0 = 0
            while c0 < nch:
                g = min(G, nch - c0)
                w = min(M, (c0 + g) * CW) - c0 * CW
                if w < g * CW and g > 1:
                    g -= 1
                    w = g * CW
                out.append(dict(src=src, dst=dst, K=K, CW=CW, nk=nk, g=g, w=w, c0=c0, row0=row0, dch0=dch0))
                c0 += g
            return out

        def slab_load(sl, stg_t, key):
            nk, w = sl["nk"], sl["w"]
            sv = stg_t[:, 0:nk * w].rearrange("p (k m) -> p k m", k=nk)
            srcv = sl["src"][sl["row0"]:sl["row0"] + sl["K"], :].rearrange("(k p) m -> p k m", p=128)[:, :, sl["c0"] * sl["CW"]:sl["c0"] * sl["CW"] + w]
            sc.dma("sp", sv, srcv, writes=[key], key="k_" + key)

        def slab_cast(sl, stg_t, skey, stb_t, bkey, en, pieces=1):
            nk, w, g, CW = sl["nk"], sl["w"], sl["g"], sl["CW"]
            cw = w // g
            sv = stg_t[:, 0:nk * w].rearrange("p (k m) -> p k m", k=nk)
            ov = stb_t[:, 0:g * nk * CW].rearrange("p (g k c) -> p g k c", g=g, k=nk)[:, :, :, 0:cw]
            iv = sv.rearrange("p k (g c) -> p g k c", g=g)
            pieces = min(pieces, nk)
            step = (nk + pieces - 1) // pieces
            for k0 in range(0, nk, step):
                k1 = min(nk, k0 + step)
                o_, i_ = ov[:, :, k0:k1, :], iv[:, :, k0:k1, :]
                if en == "act":
                    sc.op("act", lambda e: e.activation(out=o_, in_=i_, func=AF.Copy), reads=[skey], writes=[bkey])
                else:
                    sc.op(en, lambda e: e.tensor_copy(out=o_, in_=i_), reads=[skey], writes=[bkey])

        def slab_store(sl, stb_t, bkey):
            nk, g, CW = sl["nk"], sl["g"], sl["CW"]
            dv = sl["dst"][sl["dch0"] + sl["c0"]:sl["dch0"] + sl["c0"] + g].rearrange("g p x -> p g x")
            sc.dma("pool", dv, stb_t[:, 0:g * nk * CW].rearrange("p (g x) -> p g x", g=g), reads=[bkey], writes=["pk"], key="k_" + bkey)

        def conv_specs(names, cap):
            L = []
            for nm in names:
                if nm == "l1_pool_w":
                    for g in range(4):
                        L += slab_list(Wd[nm], Pk[nm], 512, 512, 128, cap, row0=g * 512, dch0=g * 4)
                else:
                    K_, M_ = WSHAPES[nm]
                    L += slab_list(Wd[nm], Pk[nm], K_, M_, 192 if nm == "l0_b_w_uq" else 128, cap)
            return L

        EARLY = ["l0_ffn1_w_gate", "l0_ffn1_w_up", "l0_ffn1_w_down", "l0_w_in", "l0_b_w_uq", "l0_b_w_ukv"]
        LATE = [p + sfx for p in FFNS[1:] for sfx in ("_w_gate", "_w_up", "_w_down")] + ["l0_w_out", "l1_pool_w"]

        with ExitStack() as ph:
            stg = [ph.enter_context(nc.sbuf_tensor(f"stg{i}", [128, 8192], F32)) for i in range(2)]
            stb = [ph.enter_context(nc.sbuf_tensor(f"stb{i}", [128, 8192], BF16)) for i in range(2)]
            for n_, sl in enumerate(conv_specs(EARLY, 8192)):
                i = n_ % 2
                slab_load(sl, stg[i], f"stg{i}")
                slab_cast(sl, stg[i], f"stg{i}", stb[i], f"stb{i}", "act" if n_ % 2 else "dve")
                slab_store(sl, stb[i], f"stb{i}")
            sc.barrier()

        if stop_after < 1:
            sc.barrier()
            return nc

        def alloc_ffn(ph, tg):
            B = {}
            B["xres"] = ph.enter_context(nc.sbuf_tensor("xres" + tg, [128, DC, T], F32))
            B["xn"] = ph.enter_context(nc.sbuf_tensor("xn" + tg, [128, DC, T], BF16))
            B["hT"] = ph.enter_context(nc.sbuf_tensor("hT" + tg, [128, FC, T], BF16))
            B["y"] = ph.enter_context(nc.sbuf_tensor("ybuf" + tg, [128, DC, T], F32))
            B["wg"] = [ph.enter_context(nc.sbuf_tensor(f"wg{i}" + tg, [128, DC, 128], BF16)) for i in range(2)]
            B["wu"] = [ph.enter_context(nc.sbuf_tensor(f"wu{i}" + tg, [128, DC, 128], BF16)) for i in range(2)]
            B["wd"] = [ph.enter_context(nc.sbuf_tensor(f"wd{i}" + tg, [128, 6144], BF16)) for i in range(2)]
            B["sm"] = [ph.enter_context(nc.sbuf_tensor(f"sm{i}" + tg, [128, T], F32)) for i in range(16)]
            B["sqi"] = 0
            B["wgi"] = 0
            B["wdi"] = 0
            return B

        XR = [("xres", c) for c in range(DC)]
        XN = [("xn", c) for c in range(DC)]
        YK = [("y", c) for c in range(DC)]
        HK = [("hT", j) for j in range(FC)]

        def rstd_from_sumsq(B, ps_i, n, out_i, half=False, npart=128):
            sm = B["sm"]
            sc.op("dve", lambda e: e.tensor_scalar(out=sm[3][0:npart, :], in0=PS[ps_i][0:npart, :], scalar1=1.0 / n, scalar2=EPS,
                                                   op0=ALU.mult, op1=ALU.add), reads=[PK[ps_i]], writes=["sm3"])
            sc.op("act", lambda e: e.activation(out=sm[4][0:npart, :], in_=sm[3][0:npart, :], func=AF.Sqrt,
                                                scale=(4.0 if half else 1.0)), reads=["sm3"], writes=["sm4"])
            sc.op("dve", lambda e: e.reciprocal(out=sm[out_i][0:npart, :], in_=sm[4][0:npart, :]), reads=["sm4"], writes=[f"sm{out_i}"])

        SQR = [0, 1, 14, 15]

        def sq_issue(B, src_ap, skeys, first, scale=None):
            sm = B["sm"]
            kw = {} if scale is None else {"scale": scale}
            if first:
                sc.op("act", lambda e: e.activation(out=sm[2][:], in_=src_ap, func=AF.Square, **kw), reads=skeys, writes=["sm2"])
                return None
            i = SQR[B["sqi"] % 4]
            B["sqi"] += 1
            sc.op("act", lambda e: e.activation(out=sm[i][:], in_=src_ap, func=AF.Square, **kw), reads=skeys, writes=[f"sm{i}"])
            return i

        def sq_add(B, i):
            sm = B["sm"]
            if i is None:
                return
            sc.op("dve", lambda e: e.tensor_tensor(out=sm[2][:], in0=sm[2][:], in1=sm[i][:], op=ALU.add), reads=[f"sm{i}", "sm2"], writes=["sm2"])

        def sq_acc(B, src_ap, skeys, first, scale=None):
            sq_add(B, sq_issue(B, src_ap, skeys, first, scale))

        def finish_stats(B, n, out_i, half=False, ps_i=7):
            sm = B["sm"]
            sc.op("pe", lambda e: e.matmul(PS[ps_i][:], lhsT=ONES, rhs=sm[2][:], start=True, stop=True), reads=["sm2", "cmat"], writes=[PK[ps_i]])
            rstd_from_sumsq(B, ps_i, n, out_i, half)

        def norm_stats(B, src, keys, nch, n, out_i, half=False, ps_i=7):
            for c in range(nch):
                sq_acc(B, src[:, c, :], [keys[c]], c == 0)
            finish_stats(B, n, out_i, half, ps_i)

        def apply_norm(B, src, skeys, gname, rs_i, dst, dkeys, nch=DC):
            sm = B["sm"]
            for c in range(nch):
                sc.op("dve", lambda e: e.scalar_tensor_tensor(out=dst[:, c, :], in0=src[:, c, :], scalar=gv(gname, c), in1=sm[rs_i][:],
                                                              op0=ALU.mult, op1=ALU.mult), reads=[skeys[c], f"sm{rs_i}", "cvec"], writes=[dkeys[c]])

        LAG = 3

        def ffn_prep_xn(B, p, c):
            xres, xn = B["xres"], B["xn"]
            sc.op("act", lambda e: e.activation(out=xn[:, c, :], in_=xres[:, c, :], func=AF.Identity, scale=gv(p + "_pre_g", c)),
                  reads=[XR[c], "cvec"], writes=[XN[c]])

        def tail(B, rs_i, next_p=None, stats=False):
            y, xres, sm = B["y"], B["xres"], B["sm"]
            slots = {}
            for c in range(DC + LAG):
                if c < DC:
                    sc.op("dve", lambda e: e.tensor_tensor(out=y[:, c, :], in0=y[:, c, :], in1=sm[rs_i][:], op=ALU.mult),
                          reads=[YK[c], f"sm{rs_i}"], writes=[YK[c]])
                    sc.op("pool", lambda e: e.tensor_tensor(out=xres[:, c, :], in0=xres[:, c, :], in1=y[:, c, :], op=ALU.add),
                          reads=[YK[c], XR[c]], writes=[XR[c]])
                    if next_p is not None:
                        ffn_prep_xn(B, next_p, c)
                    if next_p is not None or stats:
                        slots[c] = sq_issue(B, xres[:, c, :], [XR[c]], c == 0)
                if c - LAG >= 0 and (next_p is not None or stats):
                    sq_add(B, slots[c - LAG])
            if next_p is not None or stats:
                finish_stats(B, D, 5)

        def load_w(B, which, src_chunk):
            shp = list(src_chunk.shape)
            n = int(np.prod(shp[1:]))
            if which == "d":
                i = B["wdi"] % 2
                B["wdi"] += 1
                sc.dma("sp", B["wd"][i][:, 0:n], src_chunk, writes=[f"wd{i}"], key=f"k_wd{i}")
                return B["wd"][i], f"wd{i}"
            i = B["wgi"] % 4
            B["wgi"] += 1
            t = B["wg"][i // 2] if i % 2 == 0 else B["wu"][i // 2]
            k = f"wgu{i}"
            ov = t[:].rearrange("p k c -> p (k c)")[:, 0:n]
            if len(shp) == 3:
                ov = ov.rearrange("p (m x) -> p m x", m=shp[1])
            sc.dma("sp", ov, src_chunk, writes=[k], key="k_" + k)
            return t, k

        def ffn(B, p):
            xres, xn, hT, y, sm = B["xres"], B["xn"], B["hT"], B["y"], B["sm"]
            for j in range(FC):
                wg, kg = load_w(B, "g", Pk[p + "_w_gate"][j])
                wu, ku = load_w(B, "g", Pk[p + "_w_up"][j])
                pg, pu = (j % 2), 2 + (j % 2)
                for kc in range(DC):
                    sc.op("pe", lambda e: e.matmul(PS[pg][:], lhsT=wg[:, kc, :], rhs=xn[:, kc, :], start=(kc == 0), stop=(kc == DC - 1)),
                          reads=[kg, XN[kc]], writes=[PK[pg]], inc=(kc == DC - 1))
                for kc in range(DC):
                    sc.op("pe", lambda e: e.matmul(PS[pu][:], lhsT=wu[:, kc, :], rhs=xn[:, kc, :], start=(kc == 0), stop=(kc == DC - 1)),
                          reads=[ku, XN[kc]], writes=[PK[pu]], inc=(kc == DC - 1))
                a, b_, c_ = 8 + j % 2, 6 + j % 2, 10 + j % 2
                sc.op("dve", lambda e: e.tensor_tensor(out=sm[a][:], in0=PS[pg][:], in1=sm[5][:], op=ALU.mult), reads=[PK[pg], "sm5"], writes=[f"sm{a}"])
                sc.op("act", lambda e: e.activation(out=sm[b_][:], in_=sm[a][:], func=AF.Silu), reads=[f"sm{a}"], writes=[f"sm{b_}"])
                sc.op("pool", lambda e: e.tensor_tensor(out=sm[c_][:], in0=sm[b_][:], in1=sm[5][:], op=ALU.mult), reads=[f"sm{b_}", "sm5"], writes=[f"sm{c_}"])
                sc.op("dve", lambda e: e.tensor_tensor(out=hT[:, j, :], in0=sm[c_][:], in1=PS[pu][:], op=ALU.mult),
                      reads=[f"sm{c_}", PK[pu]], writes=[HK[j]])
            for m in range(DC):
                wd, kd = load_w(B, "d", Pk[p + "_w_down"][m])
                pd = 4 + (m % 2)
                for j in range(FC):
                    sc.op("pe", lambda e: e.matmul(PS[pd][:], lhsT=wd[:, j * 128:(j + 1) * 128], rhs=hT[:, j, :], start=(j == 0), stop=(j == FC - 1)),
                          reads=[kd, HK[j]], writes=[PK[pd]], inc=(j == FC - 1))
                sc.op("act", lambda e: e.activation(out=y[:, m, :], in_=PS[pd][:], func=AF.Identity, scale=gv(p + "_post_g", m)),
                      reads=[PK[pd], "cvec"], writes=[YK[m]])
                sq_acc(B, PS[pd][:], [PK[pd]], m == 0)
            finish_stats(B, D, 12, half=True)

        def load_x_tokmajor(B, t):
            y, xres = B["y"], B["xres"]
            for s4 in range(4):
                stv = y[:, 4 * s4:4 * s4 + 4, :].rearrange("p c t -> p (c t)")
                sc.dma("pool", stv, x_in[t * T + s4 * 128: t * T + (s4 + 1) * 128, :], writes=YK[4 * s4:4 * s4 + 4], key=f"k_xs{s4}")
            slots = {}
            for c in range(DC + LAG):
                if c < DC:
                    pi = c % 2
                    for s4 in range(4):
                        stv = y[:, 4 * s4:4 * s4 + 4, :].rearrange("p c t -> p (c t)")
                        sc.op("pe", lambda e: e.transpose(out=PS[pi][:, s4 * 128:(s4 + 1) * 128], in_=stv[:, c * 128:(c + 1) * 128], identity=IDENT),
                              reads=YK[4 * s4:4 * s4 + 4] + ["cmat"], writes=[PK[pi]], inc=(s4 == 3))
                    sc.op("dve", lambda e: e.tensor_copy(out=xres[:, c, :], in_=PS[pi][:]), reads=[PK[pi]], writes=[XR[c]])
                    ffn_prep_xn(B, "l0_ffn1", c)
                    slots[c] = sq_issue(B, xres[:, c, :], [XR[c]], c == 0)
                if c - LAG >= 0:
                    sq_add(B, slots[c - LAG])
            finish_stats(B, D, 5)

        def store_fm(B, src, keys, dst, t, tag):
            sc.dma("pool", dst[:, :, t * T:(t + 1) * T].rearrange("c p t -> p c t"), src[:], reads=keys, writes=[tag], key="k_st_" + tag)

        def load_fm(B, dstt, keys, src, t, tag):
            sc.dma("pool", dstt[:], src[:, :, t * T:(t + 1) * T].rearrange("c p t -> p c t"), reads=[tag], writes=keys, key="k_ld_" + tag)

        with ExitStack() as ph:
            B = alloc_ffn(ph, "_a")
            sm, y, hT, xn, xres = B["sm"], B["y"], B["hT"], B["xn"], B["xres"]
            Win = Pk["l0_w_in"]
            for t in range(NT):
                tsl = slice(t * T, (t + 1) * T)
                load_x_tokmajor(B, t)
                ffn(B, "l0_ffn1")
                tail(B, 12, stats=True)
                store_fm(B, xres, XR, X1, t, "X1")
                if stop_after < 2:
                    continue
                sc.dma("pool", sm[8][:], ropeA_d[0, :, tsl], writes=["sm8"], key="k_r0")
                sc.dma("pool", sm[9][:], ropeA_d[1, :, tsl], writes=["sm9"], key="k_r1")
                sc.dma("pool", sm[10][0:64, :], ropeB_d[0, :, tsl], writes=["sm10"], key="k_r2")
                sc.dma("pool", sm[11][0:64, :], ropeB_d[1, :, tsl], writes=["sm11"], key="k_r3")
                B["wdi"] = 0
                sc.dma("sp", B["wd"][0][:, 0:8 * 768].rearrange("p (h x) -> p h x", h=8), Pk["l0_b_w_uq"].rearrange("h p x -> p h x"),
                       writes=["wd0"], key="k_wd0")
                sc.dma("sp", B["wd"][1][:, 0:16 * 256].rearrange("p (h x) -> p h x", h=16), Pk["l0_b_w_ukv"].rearrange("h p x -> p h x"),
                       writes=["wd1"], key="k_wd1")
                wuq = B["wd"][0][:, 0:8 * 768].rearrange("p (h k c) -> p h k c", h=8, k=4)
                wukv = B["wd"][1][:, 0:16 * 256].rearrange("p (m k c) -> p m k c", m=16, k=2)
                apply_norm(B, xres, XR, "l0_mix_pre_g", 5, xn, XN)

                def proj(mc, ps_i, M=128):
                    w, kw = load_w(B, "g", Win[mc])
                    for kc in range(DC):
                        sc.op("pe", lambda e: e.matmul(PS[ps_i][0:M, :], lhsT=w[:, kc, 0:M], rhs=xn[:, kc, :], start=(kc == 0), stop=(kc == DC - 1)),
                              reads=[kw, XN[kc]], writes=[PK[ps_i]], inc=(kc == DC - 1))

                def rope(src_ap, skey, np_, perm, ci, si, dst_ap, dkey, ytmp):
                    sc.op("pe", lambda e: e.matmul(PS[6][0:np_, :], lhsT=perm, rhs=src_ap, start=True, stop=True), reads=[skey, "cmat"], writes=[PK[6]])
                    sc.op("pool", lambda e: e.tensor_tensor(out=y[0:np_, ytmp, :], in0=src_ap, in1=sm[ci][0:np_, :], op=ALU.mult),
                          reads=[skey, f"sm{ci}"], writes=[YK[ytmp]])
                    sc.op("dve", lambda e: e.tensor_tensor(out=y[0:np_, ytmp + 1, :], in0=PS[6][0:np_, :], in1=sm[si][0:np_, :], op=ALU.mult),
                          reads=[PK[6], f"sm{si}"], writes=[YK[ytmp + 1]])
                    sc.op("dve", lambda e: e.tensor_tensor(out=dst_ap, in0=y[0:np_, ytmp, :], in1=y[0:np_, ytmp + 1, :], op=ALU.add),
                          reads=[YK[ytmp], YK[ytmp + 1]], writes=[dkey])

                def a_stage0(mc):
                    pi = mc % 3
                    yr = 4 * (mc % 3)
                    proj(mc, pi)
                    sc.op("act", lambda e: e.activation(out=y[:, yr, :], in_=PS[pi][:], func=AF.Copy), reads=[PK[pi]], writes=[YK[yr]])
                    sq = 2 if mc % 2 == 0 else 13
                    sc.op("act", lambda e: e.activation(out=sm[sq][:], in_=PS[pi][:], func=AF.Square), reads=[PK[pi]], writes=[f"sm{sq}"])

                def a_stage1(mc):
                    yr = 4 * (mc % 3)
                    sq = 2 if mc % 2 == 0 else 13
                    sc.op("pe", lambda e: e.matmul(PS[7][:], lhsT=ONES, rhs=sm[sq][:], start=True, stop=True), reads=[f"sm{sq}", "cmat"], writes=[PK[7]])
                    rstd_from_sumsq(B, 7, 128, 5)
                    gname = "l0_a_q_norm_g" if mc < 8 else "l0_a_k_norm_g"
                    sc.op("dve", lambda e: e.scalar_tensor_tensor(out=y[:, yr + 1, :], in0=y[:, yr, :], scalar=gv(gname, 0), in1=sm[5][:],
                                                                  op0=ALU.mult, op1=ALU.mult), reads=[YK[yr], "sm5", "cvec"], writes=[YK[yr + 1]])

                def a_stage2(mc):
                    yr = 4 * (mc % 3)
                    ob = hT[:, mc, :]
                    rope(y[:, yr + 1, :], YK[yr + 1], 128, PERMA, 8, 9, ob, HK[mc], yr + 2)
                    dst = QA[mc, :, tsl] if mc < 8 else KA[mc - 8, :, tsl]
                    sc.dma("pool", dst, ob, reads=[HK[mc]], writes=["QKA"], key=f"k_o{mc % 4}")

                for i in range(12):
                    if i < 10:
                        a_stage0(i)
                    if 0 <= i - 1 < 10:
                        a_stage1(i - 1)
                    if 0 <= i - 2 < 10:
                        a_stage2(i - 2)
                def lowrank(mcs, gname, n, ybase, hbase):
                    nch = len(mcs)
                    for i, mc in enumerate(mcs):
                        pi = i % 2
                        proj(mc, pi)
                        sc.op("act", lambda e: e.activation(out=y[:, ybase + i, :], in_=PS[pi][:], func=AF.Copy), reads=[PK[pi]], writes=[YK[ybase + i]])
                    norm_stats(B, y[:, ybase:ybase + nch, :], YK[ybase:ybase + nch], nch, n, 5)
                    for i in range(nch):
                        sc.op("dve", lambda e: e.scalar_tensor_tensor(out=hT[:, hbase + i, :], in0=y[:, ybase + i, :], scalar=gv(gname, i), in1=sm[5][:],
                                                                      op0=ALU.mult, op1=ALU.mult), reads=[YK[ybase + i], "sm5", "cvec"], writes=[HK[hbase + i]])

                lowrank([12, 13, 14, 15], "l0_b_cq_norm_g", 512, 8, 12)
                for vi in range(2):
                    w, kw = load_w(B, "g", Win[10 + vi])
                    for s4 in range(4):
                        for kc in range(DC):
                            sc.op("pe", lambda e: e.matmul(PS[vi][:, s4 * 128:(s4 + 1) * 128], lhsT=xn[:, kc, s4 * 128:(s4 + 1) * 128], rhs=w[:, kc, :],
                                                           start=(kc == 0), stop=(kc == DC - 1)), reads=[kw, XN[kc]], writes=[PK[vi]],
                                  inc=(kc == DC - 1 and s4 == 3))
                    ob = hT[:, 10 + vi, :]
                    sc.op("act", lambda e: e.activation(out=ob, in_=PS[vi][:], func=AF.Copy), reads=[PK[vi]], writes=[HK[10 + vi]])
                    sc.dma("pool", VA[tsl, vi * 128:(vi + 1) * 128].rearrange("(s p) d -> p s d", p=128),
                           ob.rearrange("p (s d) -> p s d", s=4), reads=[HK[10 + vi]], writes=["VA"], key=f"k_o{vi}")

                lowrank([16, 17], "l0_b_ckv_norm_g", 256, 12, 16)
                proj(18, 4, M=64)
                sc.op("act", lambda e: e.activation(out=y[0:64, 0, :], in_=PS[4][0:64, :], func=AF.Copy), reads=[PK[4]], writes=[YK[0]])
                ob2 = hT[0:64, 36, :]
                rope(y[0:64, 0, :], YK[0], 64, PERMB, 10, 11, ob2, HK[36], 1)
                sc.dma("pool", KBr[:, tsl], ob2, reads=[HK[36]], writes=["KBr"], key="k_kr")
                for h in range(8):
                    pi = h % 2
                    for kc in range(4):
                        sc.op("pe", lambda e: e.matmul(PS[pi][:], lhsT=wuq[:, h, kc, 0:128], rhs=hT[:, 12 + kc, :], start=(kc == 0), stop=(kc == 3)),
                              reads=["wd0", HK[12 + kc]], writes=[PK[pi]], inc=(kc == 3))
                    ob = hT[:, 20 + (h % 4), :]
                    sc.op("act", lambda e: e.activation(out=ob, in_=PS[pi][:], func=AF.Copy), reads=[PK[pi]], writes=[HK[20 + h % 4]])
                    sc.dma("pool", QBn[h, :, tsl], ob, reads=[HK[20 + h % 4]], writes=["QBn"], key=f"k_o{h % 4}")
                    pj = 2 + h % 2
                    for kc in range(4):
                        sc.op("pe", lambda e: e.matmul(PS[pj][0:64, :], lhsT=wuq[:, h, kc, 128:192], rhs=hT[:, 12 + kc, :], start=(kc == 0), stop=(kc == 3)),
                              reads=["wd0", HK[12 + kc]], writes=[PK[pj]], inc=(kc == 3))
                    yr = 4 * (h % 2)
                    sc.op("act", lambda e: e.activation(out=y[0:64, yr, :], in_=PS[pj][0:64, :], func=AF.Copy), reads=[PK[pj]], writes=[YK[yr]])
                    ob2 = hT[0:64, 24 + (h % 4), :]
                    rope(y[0:64, yr, :], YK[yr], 64, PERMB, 10, 11, ob2, HK[24 + h % 4], yr + 1)
                    sc.dma("pool", QBr[h, :, tsl], ob2, reads=[HK[24 + h % 4]], writes=["QBr"], key=f"k_p{h % 4}")
                for h in range(8):
                    pi = h % 2
                    for kc in range(2):
                        sc.op("pe", lambda e: e.matmul(PS[pi][:], lhsT=wukv[:, 2 * h, kc, :], rhs=hT[:, 16 + kc, :], start=(kc == 0), stop=(kc == 1)),
                              reads=["wd1", HK[16 + kc]], writes=[PK[pi]], inc=(kc == 1))
                    ob = hT[:, 28 + (h % 4), :]
                    sc.op("act", lambda e: e.activation(out=ob, in_=PS[pi][:], func=AF.Copy), reads=[PK[pi]], writes=[HK[28 + h % 4]])
                    sc.dma("pool", KBn[h, :, tsl], ob, reads=[HK[28 + h % 4]], writes=["KBn"], key=f"k_q{h % 4}")
                for s4 in range(4):
                    for hh in range(2):
                        pi = 2 + hh
                        for kc in range(2):
                            sc.op("pe", lambda e: e.matmul(PS[pi][:].rearrange("p (h c) -> p h c", h=4), lhsT=hT[:, 16 + kc, s4 * 128:(s4 + 1) * 128],
                                                           rhs=wukv[:, 8 * hh + 1:8 * hh + 8:2, kc, :], start=(kc == 0), stop=(kc == 1)),
                                  reads=["wd1", HK[16 + kc]], writes=[PK[pi]], inc=(kc == 1))
                        ob = hT[:, 32 + 2 * (s4 % 2) + hh, :]
                        hk = HK[32 + 2 * (s4 % 2) + hh]
                        sc.op("act", lambda e: e.activation(out=ob, in_=PS[pi][:], func=AF.Copy), reads=[PK[pi]], writes=[hk])
                        sc.dma("pool", VB[t * T + s4 * 128:t * T + (s4 + 1) * 128, hh * 512:(hh + 1) * 512], ob, reads=[hk], writes=["VB"],
                               key=f"k_v{(2 * s4 + hh) % 4}")
            sc.barrier()

        if stop_after < 3:
            sc.barrier()
            return nc

        with ExitStack() as ph:
            KT = [ph.enter_context(nc.sbuf_tensor(f"KT{i}", [128, S], BF16)) for i in range(2)]
            VT = [ph.enter_context(nc.sbuf_tensor(f"VT{i}", [128, NKC, 128], BF16)) for i in range(2)]
            KR = ph.enter_context(nc.sbuf_tensor("KR", [128, S], BF16))
            QT = [ph.enter_context(nc.sbuf_tensor(f"QT{i}", [128, T], BF16)) for i in range(3)]
            QR = [ph.enter_context(nc.sbuf_tensor(f"QR{i}", [128, T], BF16)) for i in range(3)]
            PT = [ph.enter_context(nc.sbuf_tensor(f"PT{i}", [128, T], BF16)) for i in range(8)]
            RI = [ph.enter_context(nc.sbuf_tensor(f"RI{i}", [128, T], F32)) for i in range(2)]
            OB = [ph.enter_context(nc.sbuf_tensor(f"OB{i}", [128, T], BF16)) for i in range(2)]
            ACC = [ph.enter_context(nc.sbuf_tensor(f"ACC{i}", [128, T], F32)) for i in range(8)]
            sc.op("pool", lambda e: e.memset(KR[64:128, :], 0.0), writes=["KR"])
            for i in range(3):
                sc.op("pool", lambda e: e.memset(QR[i][64:128, :], 0.0), writes=[f"QR{i}"])
            sc.dma("pool", KR[0:64, :], KBr, reads=["KBr"], writes=["KR"], key="k_KR")
            NQT = S // T
            items = [(hd, qt) for hd in range(16) for qt in range(NQT)]
            LCAP = 5632
            stgL = [ph.enter_context(nc.sbuf_tensor(f"stgL{i}", [128, LCAP], F32)) for i in range(2)]
            stbL = [ph.enter_context(nc.sbuf_tensor(f"stbL{i}", [128, LCAP], BF16)) for i in range(2)]
            late = conv_specs(LATE, LCAP)
            per_item = (len(late) + len(items) - 1) // len(items)
            lpos = [0, 0]

            def late_step(k):
                for _ in range(k):
                    if lpos[1] < lpos[0]:
                        j = lpos[1]
                        i = j % 2
                        slab_cast(late[j], stgL[i], f"stgL{i}", stbL[i], f"stbL{i}", "act", pieces=2)
                        slab_store(late[j], stbL[i], f"stbL{i}")
                        lpos[1] += 1
                    if lpos[0] < len(late) and lpos[0] - lpos[1] < 2:
                        j = lpos[0]
                        i = j % 2
                        slab_load(late[j], stgL[i], f"stgL{i}")
                        lpos[0] += 1

            def head_loads(hd):
                h, b = hd % 8, hd % 2
                if hd < 8:
                    ksrc = KA[h // 4]
                    vsrc = VA[:, (h // 4) * 128:(h // 4 + 1) * 128].rearrange("(c p) d -> p c d", p=128)
                else:
                    ksrc = KBn[h]
                    vsrc = VB[:, h * 128:(h + 1) * 128].rearrange("(c p) d -> p c d", p=128)
                sc.dma("sp", KT[b][:], ksrc, writes=[f"KT{b}"], key=f"k_KT{b}")
                for v4 in range(0, NKC, 16):
                    v5 = min(NKC, v4 + 16)
                    sc.dma("sp", VT[b][:, v4:v5, :], vsrc[:, v4:v5, :], writes=[f"VT{b}"], key=f"k_VT{b}")

            def q_load(n):
                hd, qt = items[n]
                h, q3 = hd % 8, n % 3
                qsl = slice(qt * T, (qt + 1) * T)
                sc.dma("pool", QT[q3][:], (QBn if hd >= 8 else QA)[h, :, qsl], writes=[f"QT{q3}"], key=f"k_QT{q3}")
                if hd >= 8:
                    sc.dma("pool", QR[q3][0:64, :], QBr[h, :, qsl], writes=[f"QR{q3}"], key=f"k_QR{q3}")

            head_loads(0)
            q_load(0)
            pending = [None]
            for n, (hd, qt) in enumerate(items):
                isB = hd >= 8
                h, b = hd % 8, hd % 2
                scale = (192.0 if isB else 128.0) ** -0.5
                q3, o2 = n % 3, n % 2
                qsl = slice(qt * T, (qt + 1) * T)
                if qt == 0 and hd + 1 < 16:
                    head_loads(hd + 1)
                if n + 1 < len(items):
                    q_load(n + 1)
                late_step(per_item)
                pO, pL = 4 + o2, 6 + o2
                NSB, LA = 4, 3
                st = {"d": 0, "l": 0}

                def s_mm(kc):
                    pi = kc % NSB
                    pt = kc % 8
                    ksl = slice(kc * 128, (kc + 1) * 128)
                    if isB:
                        sc.op("pe", lambda e: e.matmul(PS[pi][:], lhsT=KT[b][:, ksl], rhs=QT[q3][:], start=True, stop=False),
                              reads=[f"KT{b}", f"QT{q3}"], writes=[PK[pi]], inc=False)
                        sc.op("pe", lambda e: e.matmul(PS[pi][:], lhsT=KR[:, ksl], rhs=QR[q3][:], start=False, stop=True),
                              reads=["KR", f"QR{q3}"], writes=[PK[pi]])
                    else:
                        sc.op("pe", lambda e: e.matmul(PS[pi][:], lhsT=KT[b][:, ksl], rhs=QT[q3][:], start=True, stop=True),
                              reads=[f"KT{b}", f"QT{q3}"], writes=[PK[pi]])
                    sc.op("act", lambda e: e.activation(out=PT[pt][:], in_=PS[pi][:], func=AF.Exp, scale=scale), reads=[PK[pi]], writes=[f"PT{pt}"])

                def pv_mm(kc):
                    pt = kc % 8
                    sc.op("pe", lambda e: e.matmul(PS[pO][:], lhsT=VT[b][:, kc, :], rhs=PT[pt][:], start=(kc == 0), stop=(kc == NKC - 1)),
                          reads=[f"VT{b}", f"PT{pt}"], writes=[PK[pO]], inc=True)
                    if (not isB) and kc % 3 == 2:
                        sc.op("pe", lambda e: e.matmul(PS[pL][:], lhsT=onesb[:], rhs=PT[pt][:], start=(st["l"] == 0), stop=False),
                              reads=["onesb", f"PT{pt}"], writes=[PK[pL]], inc=True)
                        st["l"] += 1
                        return
                    a = 4 * o2 + st["d"] % 4
                    if st["d"] < 4:
                        sc.op("dve", lambda e: e.tensor_copy(out=ACC[a][:], in_=PT[pt][:]), reads=[f"PT{pt}"], writes=[f"ACC{a}"])
                    else:
                        sc.op("dve", lambda e: e.tensor_tensor(out=ACC[a][:], in0=ACC[a][:], in1=PT[pt][:], op=ALU.add),
                              reads=[f"PT{pt}", f"ACC{a}"], writes=[f"ACC{a}"])
                    st["d"] += 1

                def make_epi(hd=hd, qsl=qsl, o2=o2, pO=pO, pL=pL, st=st):
                    def epi():
                        na = min(4, st["d"])
                        for a4 in range(na):
                            a = 4 * o2 + a4
                            sc.op("pe", lambda e: e.matmul(PS[pL][:], lhsT=ONES, rhs=ACC[a][:], start=(a4 == 0 and st["l"] == 0), stop=(a4 == na - 1)),
                                  reads=["cmat", f"ACC{a}"], writes=[PK[pL]], inc=(a4 == na - 1))
                        sc.op("dve", lambda e: e.reciprocal(out=RI[o2][:], in_=PS[pL][:]), reads=[PK[pL]], writes=[f"RI{o2}"])
                        sc.op("dve", lambda e: e.tensor_tensor(out=OB[o2][:], in0=PS[pO][:], in1=RI[o2][:], op=ALU.mult),
                              reads=[PK[pO], f"RI{o2}"], writes=[f"OB{o2}"])
                        sc.dma("pool", ATT[hd, :, qsl], OB[o2][:], reads=[f"OB{o2}"], writes=["ATT"], key=f"k_OB{o2}")
                    return epi

                for kc in range(min(LA, NKC)):
                    s_mm(kc)
                for kc in range(NKC):
                    if kc + LA < NKC:
                        s_mm(kc + LA)
                    pv_mm(kc)
                    if kc == min(5, NKC - 2) and pending[0] is not None:
                        pending[0]()
                        pending[0] = None
                if pending[0] is not None:
                    pending[0]()
                pending[0] = make_epi()
            pending[0]()
            while lpos[1] < len(late):
                late_step(1)
            sc.barrier()

        if stop_after < 4:
            sc.barrier()
            return nc

        with ExitStack() as ph:
            B = alloc_ffn(ph, "_b")
            sm, y, hT, xn, xres = B["sm"], B["y"], B["hT"], B["xn"], B["xres"]
            for t in range(NT):
                load_fm(B, xres, XR, X1, t, "X1")
                load_fm(B, xn, XN, ATT, t, "ATT")
                for m in range(DC):
                    w, kw = load_w(B, "g", Pk["l0_w_out"][m])
                    pi = m % 2
                    for kc in range(DC):
                        sc.op("pe", lambda e: e.matmul(PS[pi][:], lhsT=w[:, kc, :], rhs=xn[:, kc, :], start=(kc == 0), stop=(kc == DC - 1)),
                              reads=[kw, XN[kc]], writes=[PK[pi]], inc=(kc == DC - 1))
                    sc.op("act", lambda e: e.activation(out=y[:, m, :], in_=PS[pi][:], func=AF.Identity, scale=gv("l0_mix_post_g", m)),
                          reads=[PK[pi], "cvec"], writes=[YK[m]])
                    sq_acc(B, PS[pi][:], [PK[pi]], m == 0)
                finish_stats(B, D, 12)
                tail(B, 12, next_p="l0_ffn2")
                ffn(B, "l0_ffn2")
                tail(B, 12, next_p="l1_ffn1")
                ffn(B, "l1_ffn1")
                tail(B, 12, stats=True)
                store_fm(B, xres, XR, X4, t, "X4")
                for c in range(DC):
                    sc.op("dve", lambda e: e.scalar_tensor_tensor(out=y[:, c, :], in0=xres[:, c, :], scalar=gv("l1_mix_pre_g", c), in1=sm[5][:],
                                                                  op0=ALU.mult, op1=ALU.mult), reads=[XR[c], "sm5", "cvec"], writes=[YK[c]])
                store_fm(B, y, YK, HN4, t, "HN4")
            sc.barrier()

            if stop_after < 5:
                sc.barrier()
                return nc

            HW = T + 16
            hf = hT[:].rearrange("p c t -> p (c t)").bitcast(F32)
            REG = [hf[:, r * 2560:r * 2560 + 4 * HW].rearrange("p (c t) -> p c t", c=4) for r in range(3)]
            RK = [HK[10 * r:10 * r + 9] for r in range(3)]
            TMP = hf[:, 7680:7680 + 4 * T].rearrange("p (c t) -> p c t", c=4)
            TK4 = [HK[30 + 2 * c4:32 + 2 * c4] for c4 in range(4)]

            def pooling(t):
                for w4 in range(4):
                    sc.dma("pool", sm[8 + w4][:], invc_d[w4, :, t * T:(t + 1) * T], writes=[f"sm{8 + w4}"], key=f"k_r{w4}")
                for g in range(4):
                    wwin = 2 << g
                    hw = wwin // 2
                    R0, R1, R2 = REG
                    lo = t * T - 8
                    hi = (t + 1) * T + 8
                    clo, chi = max(lo, 0), min(hi, S)
                    if clo > lo:
                        sc.op("pool", lambda e: e.memset(R0[:, :, 0:clo - lo], 0.0), writes=RK[0])
                    if chi < hi:
                        sc.op("pool", lambda e: e.memset(R0[:, :, HW - (hi - chi):HW], 0.0), writes=RK[0])
                    sc.dma("pool", R0[:, :, clo - lo:chi - lo], HN4[4 * g:4 * g + 4, :, clo:chi].rearrange("c p t -> p c t"),
                           reads=["HN4"], writes=RK[0], key="k_halo")
                    cur, ck = R0, RK[0]
                    n = HW
                    step = 1
                    lvl = 0
                    while step < wwin:
                        dstR, dk = (R1, RK[1]) if lvl % 2 == 0 else (R2, RK[2])
                        n2 = n - step
                        sc.op("dve", lambda e: e.tensor_tensor(out=dstR[:, :, 0:n2], in0=cur[:, :, 0:n2], in1=cur[:, :, step:step + n2], op=ALU.add),
                              reads=ck, writes=dk)
                        cur, ck, n = dstR, dk, n2
                        step *= 2
                        lvl += 1
                    for c4 in range(4):
                        c = 4 * g + c4
                        sc.op("dve", lambda e: e.tensor_tensor(out=TMP[:, c4, :], in0=cur[:, c4, 8 - hw:8 - hw + T], in1=sm[8 + g][:], op=ALU.mult),
                              reads=ck + [f"sm{8 + g}"], writes=TK4[c4])
                        sc.op("pool", lambda e: e.tensor_tensor(out=xn[:, c, :], in0=TMP[:, c4, :], in1=R0[:, c4, 8:8 + T], op=ALU.subtract),
                              reads=TK4[c4] + RK[0], writes=[XN[c]])

            pooling(0)
            for t in range(NT):
                for g in range(4):
                    w, kw = load_w(B, "g", Pk["l1_pool_w"][4 * g:4 * g + 4].rearrange("m p x -> p m x"))
                    wv = w[:].rearrange("p k c -> p (k c)")[:, 0:2048].rearrange("p (m k c) -> p m k c", m=4, k=4)
                    for m4 in range(4):
                        m = 4 * g + m4
                        pi = m % 2
                        for kc in range(4):
                            sc.op("pe", lambda e: e.matmul(PS[pi][:], lhsT=wv[:, m4, kc, :], rhs=xn[:, 4 * g + kc, :], start=(kc == 0), stop=(kc == 3)),
                                  reads=[kw, XN[4 * g + kc]], writes=[PK[pi]], inc=(kc == 3))
                        sc.op("act", lambda e: e.activation(out=y[:, m, :], in_=PS[pi][:], func=AF.Identity, scale=pg2[:, m:m + 1]),
                              reads=[PK[pi], "pg2"], writes=[YK[m]])
                        sq_acc(B, PS[pi][:], [PK[pi]], m == 0, scale=gv("l1_pool_scale", m))
                load_fm(B, xres, XR, X4, t, "X4")
                finish_stats(B, D, 12)
                tail(B, 12, next_p="l1_ffn2")
                ffn(B, "l1_ffn2")
                tail(B, 12)
                if t + 1 < NT:
                    pooling(t + 1)
                for s4 in range(4):
                    stv = y[:, 4 * s4:4 * s4 + 4, :].rearrange("p c t -> p (c t)")
                    for c4 in range(4):
                        pi = c4 % 2
                        for cc in range(4):
                            c = 4 * c4 + cc
                            sc.op("pe", lambda e: e.transpose(out=PS[pi][:, cc * 128:(cc + 1) * 128], in_=xres[:, c, s4 * 128:(s4 + 1) * 128], identity=IDENT),
                                  reads=[XR[c], "cmat"], writes=[PK[pi]], inc=(cc == 3))
                        sc.op("act", lambda e: e.activation(out=stv[:, c4 * 512:(c4 + 1) * 512], in_=PS[pi][:], func=AF.Copy), reads=[PK[pi]], writes=[YK[4 * s4 + c4]])
                    sc.dma("pool", y_out[t * T + s4 * 128:t * T + (s4 + 1) * 128, :], stv, reads=YK[4 * s4:4 * s4 + 4], writes=["yout"], key=f"k_ys{s4}")
            sc.barrier()
        sc.barrier()
    return nc


def make_consts(S):
    cmat = np.zeros((128, 4, 128), np.float32)
    cmat[:, 0, :] = 1.0
    cmat[:, 1, :] = np.eye(128, dtype=np.float32)
    for i in range(64):
        cmat[2 * i + 1, 2, 2 * i] = -1.0
        cmat[2 * i, 2, 2 * i + 1] = 1.0
    for i in range(32):
        cmat[2 * i + 1, 3, 2 * i] = -1.0
        cmat[2 * i, 3, 2 * i + 1] = 1.0
    tt = np.arange(S)
    row = (tt // GRID_W).astype(np.float32)
    col = (tt % GRID_W).astype(np.float32)

    def tab(rot_dim):
        half = rot_dim // 2
        freqs = (np.float32(10000.0) ** (-np.arange(0, half, 2, dtype=np.float32) / np.float32(half))).astype(np.float32)
        ang = np.concatenate([row[:, None] * freqs, col[:, None] * freqs], axis=-1).astype(np.float32)
        c = np.repeat(np.cos(ang).T, 2, axis=0)
        s = np.repeat(np.sin(ang).T, 2, axis=0)
        return np.ascontiguousarray(np.stack([c, s]).astype(np.float32))

    ropeA = tab(128)
    ropeB = tab(64)
    invc = np.zeros((4, 128, S), np.float32)
    for g, w in enumerate((2, 4, 8, 16)):
        lo = np.clip(tt - w // 2, 0, S)
        hi = np.clip(tt + w // 2, 0, S)
        invc[g, :, :] = (1.0 / (hi - lo).astype(np.float32))[None, :]
    return cmat, ropeA, ropeB, invc


def make_inputs(weights, S):
    cvec = np.zeros((128, NCV), np.float32)
    for n, l in VECS:
        cvec[:, VCOL[n]:VCOL[n] + l // 128] = np.asarray(weights[n], np.float32).reshape(l // 128, 128).T
    cmat, ropeA, ropeB, invc = make_consts(S)
    base = {"cvec": cvec, "cmat": cmat, "ropeA": ropeA, "ropeB": ropeB, "invc": invc}
    for n, s in WSHAPES.items():
        base[n] = np.ascontiguousarray(np.asarray(weights[n], np.float32).reshape(s))
    return base


_NC_CACHE = {}


def run_seqs(seqs, weights, S, n_cores=8, **bk):
    key = (S, tuple(sorted(bk.items())))
    if key not in _NC_CACHE:
        _NC_CACHE[key] = build(S, **bk)
    nc = _NC_CACHE[key]
    base = make_inputs(weights, S)
    in_maps = []
    for i in range(n_cores):
        m = dict(base)
        m["x"] = np.ascontiguousarray(seqs[i % len(seqs)], dtype=np.float32)
        in_maps.append(m)
    res = run_bass_kernel_spmd(nc, in_maps, core_ids=list(range(n_cores)))
    return res


def kernel(**inputs):
    xp = np.asarray(inputs["x_prompt"], np.float32)
    xs = np.asarray(inputs["x_sample"], np.float32)
    S = xp.shape[1]
    seqs = [xp[0], xp[1], xs[0], xs[1], xs[2], xs[3]]
    res = run_seqs(seqs, inputs, S)
    outs = [res.results[i]["y"] for i in range(6)]
    return (np.stack(outs[0:2]).astype(np.float32), np.stack(outs[2:6]).astype(np.float32))
```
